# Optimizing a Trainium2 kernel written in Bass

```python
import jax, jax.numpy as jnp
from jax import lax
import numpy as np

D_MODEL = 2048
BATCH = 2
SEQ = 4096
DEPTH = 1

N_META = 16
HEAD_DIM = 128
FOX_HEADS = 8
SB_HEADS = 8
FOX_WIDTH = FOX_HEADS * HEAD_DIM
SB_WIDTH = SB_HEADS * HEAD_DIM
D_FF = 5632
Q_BLOCK = 128
RMS_EPS = 1e-6
FFN_RESIDUAL_WEIGHT = 0.5
FORGET_BIAS_INIT = 3.0

IN_SIZES = [FOX_WIDTH, FOX_WIDTH, FOX_WIDTH, FOX_HEADS,
            SB_WIDTH, SB_WIDTH, SB_WIDTH,
            D_MODEL, D_MODEL]
IN_PROJ_WIDTH = sum(IN_SIZES)
IN_SPLIT_POINTS = [int(v) for v in np.cumsum(IN_SIZES)[:-1]]

kernel_name = "hybrid_fox_stickbreak_macaron"


def _rmsnorm(x, gain):
    x32 = x.astype(jnp.float32)
    y = x32 * lax.rsqrt(jnp.mean(x32 * x32, axis=-1, keepdims=True) + RMS_EPS)
    return (y * gain.astype(jnp.float32)).astype(x.dtype)


def _swiglu(x, w_gate, w_up, w_down):
    return (jax.nn.silu(x @ w_gate) * (x @ w_up)) @ w_down


def _query_blocks(total_len):
    blocks = [(0, N_META)]
    start = N_META
    while start < total_len:
        end = min(start + Q_BLOCK, total_len)
        blocks.append((start, end))
        start = end
    return blocks


def _fox_attention(q, k, v, log_f_cum):
    L = q.shape[1]
    scale = HEAD_DIM ** -0.5
    outs = []
    for q0, q1 in _query_blocks(L):
        qb, kb, vb = q[:, q0:q1], k[:, :q1], v[:, :q1]
        logits = jnp.einsum('bqhd,bkhd->bhqk', qb, kb).astype(jnp.float32) * scale
        decay = log_f_cum[:, :, q0:q1, None] - log_f_cum[:, :, None, :q1]
        t = jnp.arange(q0, q1)[:, None]
        s = jnp.arange(q1)[None, :]
        logits = jnp.where(s <= t, logits + decay, -jnp.inf)
        p = jax.nn.softmax(logits, axis=-1)
        outs.append(jnp.einsum('bhqk,bkhd->bqhd', p.astype(vb.dtype), vb))
    return jnp.concatenate(outs, axis=1)


def _stick_breaking_attention(q, k, v):
    L = q.shape[1]
    scale = HEAD_DIM ** -0.5
    outs = []
    for q0, q1 in _query_blocks(L):
        qb, kb, vb = q[:, q0:q1], k[:, :q1], v[:, :q1]
        z = jnp.einsum('bqhd,bkhd->bhqk', qb, kb).astype(jnp.float32) * scale
        t = jnp.arange(q0, q1)[:, None]
        s = jnp.arange(q1)[None, :]
        strict = s < t
        log_beta = jax.nn.log_sigmoid(z)
        log_one_minus = jnp.where(strict, log_beta - z, 0.0)
        later = lax.cumsum(log_one_minus, axis=3, reverse=True) - log_one_minus
        w = jnp.where(strict, jnp.exp(log_beta + later), 0.0)
        outs.append(jnp.einsum('bhqk,bkhd->bqhd', w.astype(vb.dtype), vb))
    return jnp.concatenate(outs, axis=1)


def _hybrid_mixer(xn, w_in, b_forget, fox_q_norm, fox_k_norm, w_branch_fox, w_branch_sb, w_out):
    B, L, _ = xn.shape
    proj = xn @ w_in
    fq, fk, fv, f_logit, sq, sk, sv, g_fox, g_sb = jnp.split(proj, IN_SPLIT_POINTS, axis=-1)
    heads = lambda a, n: a.reshape(B, L, n, HEAD_DIM)
    fq = _rmsnorm(heads(fq, FOX_HEADS), fox_q_norm)
    fk = _rmsnorm(heads(fk, FOX_HEADS), fox_k_norm)
    log_f = jax.nn.log_sigmoid((f_logit + b_forget).astype(jnp.float32))
    log_f_cum = jnp.transpose(lax.cumsum(log_f, axis=1), (0, 2, 1))
    o_fox = _fox_attention(fq, fk, heads(fv, FOX_HEADS), log_f_cum).reshape(B, L, FOX_WIDTH)
    o_sb = _stick_breaking_attention(heads(sq, SB_HEADS), heads(sk, SB_HEADS),
                                     heads(sv, SB_HEADS)).reshape(B, L, SB_WIDTH)
    merged = jax.nn.sigmoid(g_fox) * (o_fox @ w_branch_fox) + jax.nn.sigmoid(g_sb) * (o_sb @ w_branch_sb)
    return merged @ w_out


def setup_inputs(seed: int = 0) -> dict:
    key = jax.random.key(seed)
    ks = jax.random.split(key, 20)
    f32 = jnp.float32
    nrm = lambda k, shape, fan_in: jax.random.normal(k, shape, f32) * (fan_in ** -0.5)
    gain = lambda k, shape: 1.0 + 0.02 * jax.random.normal(k, shape, f32)
    return {
        "x": jax.random.normal(ks[0], (BATCH, SEQ, D_MODEL), f32),
        "meta_tokens": jax.random.normal(ks[1], (N_META, D_MODEL), f32),
        "ffn1_norm": gain(ks[2], (DEPTH, D_MODEL)),
        "ffn1_w_gate": nrm(ks[3], (DEPTH, D_MODEL, D_FF), D_MODEL),
        "ffn1_w_up": nrm(ks[4], (DEPTH, D_MODEL, D_FF), D_MODEL),
        "ffn1_w_down": nrm(ks[5], (DEPTH, D_FF, D_MODEL), D_FF),
        "mix_norm": gain(ks[6], (DEPTH, D_MODEL)),
        "w_in": nrm(ks[7], (DEPTH, D_MODEL, IN_PROJ_WIDTH), D_MODEL),
        "b_forget": FORGET_BIAS_INIT + 0.1 * jax.random.normal(ks[8], (DEPTH, FOX_HEADS), f32),
        "fox_q_norm": gain(ks[9], (DEPTH, FOX_HEADS, HEAD_DIM)),
        "fox_k_norm": gain(ks[10], (DEPTH, FOX_HEADS, HEAD_DIM)),
        "w_branch_fox": nrm(ks[11], (DEPTH, FOX_WIDTH, D_MODEL), FOX_WIDTH),
        "w_branch_sb": nrm(ks[12], (DEPTH, SB_WIDTH, D_MODEL), SB_WIDTH),
        "w_out": nrm(ks[13], (DEPTH, D_MODEL, D_MODEL), D_MODEL),
        "ffn2_norm": gain(ks[14], (DEPTH, D_MODEL)),
        "ffn2_w_gate": nrm(ks[15], (DEPTH, D_MODEL, D_FF), D_MODEL),
        "ffn2_w_up": nrm(ks[16], (DEPTH, D_MODEL, D_FF), D_MODEL),
        "ffn2_w_down": nrm(ks[17], (DEPTH, D_FF, D_MODEL), D_FF),
    }


def reference(x, meta_tokens, ffn1_norm, ffn1_w_gate, ffn1_w_up, ffn1_w_down, mix_norm, w_in, b_forget,
              fox_q_norm, fox_k_norm, w_branch_fox, w_branch_sb, w_out, ffn2_norm, ffn2_w_gate, ffn2_w_up,
              ffn2_w_down):
    B = x.shape[0]
    meta = jnp.broadcast_to(meta_tokens[None].astype(x.dtype), (B, N_META, D_MODEL))
    h = jnp.concatenate([meta, x], axis=1)
    for layer in range(DEPTH):
        h = h + FFN_RESIDUAL_WEIGHT * _swiglu(_rmsnorm(h, ffn1_norm[layer]), ffn1_w_gate[layer],
                                              ffn1_w_up[layer], ffn1_w_down[layer])
        h = h + _hybrid_mixer(_rmsnorm(h, mix_norm[layer]), w_in[layer], b_forget[layer],
                              fox_q_norm[layer], fox_k_norm[layer], w_branch_fox[layer],
                              w_branch_sb[layer], w_out[layer])
        h = h + FFN_RESIDUAL_WEIGHT * _swiglu(_rmsnorm(h, ffn2_norm[layer]), ffn2_w_gate[layer],
                                              ffn2_w_up[layer], ffn2_w_down[layer])
    return h[:, N_META:]
```

```python
import numpy as np
import ml_dtypes
import concourse.bass as bass
import concourse.mybir as mybir
from concourse.bass_utils import run_bass_kernel_spmd

F32 = mybir.dt.float32
BF16 = mybir.dt.bfloat16
AF = mybir.ActivationFunctionType
ALU = mybir.AluOpType
AX = mybir.AxisListType

D = 2048
DC = 16
DFF = 5632
NG = DFF // 256
TO = 1024
TA = 1040
NM = 16
LSEQ = 4112
EPS = 1e-6
NEG = -30000.0
INW = 10248
C_FQ, C_FK, C_FV, C_F, C_SQ, C_SK, C_SV, C_GF, C_GS = 0, 1024, 2048, 3072, 3080, 4104, 5128, 6152, 8200


class Res:
    def __init__(self, name=""):
        self.name = name
        self.w = None
        self.r = {}
        self.dsem = None
        self.dcnt = 0


class Ctx:
    def __init__(self, nc):
        self.nc = nc
        self.E = {"pe": nc.tensor, "act": nc.scalar, "dve": nc.vector, "pool": nc.gpsimd, "sp": nc.sync}
        self.esem = {e: nc.alloc_semaphore(name="es_" + e) for e in ("pe", "act", "dve", "pool")}
        self.ecnt = {e: 0 for e in self.esem}
        self.seen = {}
        self.all_res = []
        self.semkey = {}
        self.cc_sem = nc.alloc_semaphore(name="cc_sem")
        self.cc_cnt = 0

    def res(self, name=""):
        r = Res(name)
        self.all_res.append(r)
        return r

    def _k(self, sem):
        return id(sem)

    def wait(self, eng, tok):
        if tok is None:
            return
        sem, val = tok
        if eng == "pe" and sem is self.esem["pe"]:
            return
        key = (eng, self._k(sem))
        if self.seen.get(key, 0) >= val:
            return
        self.seen[key] = val
        self.E[eng].wait_ge(sem, val)

    def mark(self, eng, ins):
        self.ecnt[eng] += 1
        ins.then_inc(self.esem[eng], 1)
        return (self.esem[eng], self.ecnt[eng])

    def pre(self, eng, reads, writes):
        for r in reads:
            self.wait(eng, r.w)
        for w in writes:
            self.wait(eng, w.w)
            for t in list(w.r.values()):
                self.wait(eng, t)

    def post(self, tok, reads, writes):
        for r in reads:
            r.r[self._k(tok[0])] = tok
        for w in writes:
            w.w = tok
            w.r = {}

    def op(self, eng, fn, reads=(), writes=(), mark=True):
        self.pre(eng, reads, writes)
        ins = fn()
        if mark:
            tok = self.mark(eng, ins)
            self.post(tok, reads, writes)
            return tok
        return None

    def dma(self, q, out, in_, reads=(), writes=(), **kw):
        self.pre(q, reads, writes)
        res = writes[0]
        if res.dsem is None:
            self.nsem = getattr(self, "nsem", 0) + 1
            res.dsem = self.nc.alloc_semaphore(name="ds%d_%s" % (self.nsem, res.name))
        res.dcnt += 16
        self.E[q].dma_start(out=out, in_=in_, **kw).then_inc(res.dsem, 16)
        tok = (res.dsem, res.dcnt)
        self.post(tok, reads, writes)
        return tok

    def collective(self, ins_ap, outs_ap, reads, writes, groups):
        self.pre("pool", reads, writes)
        self.cc_cnt += 1
        self.nc.gpsimd.collective_compute(
            "AllGather", ALU.bypass, replica_groups=groups, ins=[ins_ap], outs=[outs_ap], dma_qos="P3"
        ).then_inc(self.cc_sem, 1)
        tok = (self.cc_sem, self.cc_cnt)
        self.post(tok, reads, writes)
        return tok

    def barrier(self, skip_cc=False):
        toks = [(self.esem[e], self.ecnt[e]) for e in self.esem if self.ecnt[e] > 0]
        for r in self.all_res:
            if r.dsem is not None and r.dcnt > 0:
                toks.append((r.dsem, r.dcnt))
        if self.cc_cnt and not skip_cc:
            toks.append((self.cc_sem, self.cc_cnt))
        for eng in ("pe", "act", "dve", "pool", "sp"):
            for t in toks:
                self.wait(eng, t)


class Arena:
    def __init__(self, tens, nbytes):
        self.t = tens
        self.nbytes = nbytes

    def carve(self, off, shape, dt, parts=128):
        esz = 4 if dt == F32 else 2
        n = int(np.prod(shape)) * esz
        assert off % 4 == 0 and off + n <= self.nbytes, (off, n, self.nbytes)
        a = self.t[0:parts, off // 2:(off + n) // 2]
        if dt == F32:
            a = a.bitcast(F32)
        if len(shape) == 2:
            a = a.rearrange("p (a b) -> p a b", a=shape[0])
        elif len(shape) == 3:
            a = a.rearrange("p (a b c) -> p a b c", a=shape[0], b=shape[1])
        return a


ARENA_BYTES = 212000


def build(mode="full", ngroups=2):
    nc = bass.Bass("TRN2", target_bir_lowering=False)
    K = Ctx(nc)

    def din(name, shape, dt=F32):
        return nc.dram_tensor(name, list(shape), dt, kind="ExternalInput")

    FULL = (mode == "full")
    GROUPS = [[0, 1, 2, 3], [4, 5, 6, 7]][:ngroups]

    xT = din("xT", [D, TA])
    g1 = din("g1", [128, DC]); g2 = din("g2", [128, DC]); g3 = din("g3", [128, DC])
    gq = din("gq", [128, 8]); gk = din("gk", [128, 8]); bfo = din("bfo", [8, 1])
    if FULL:
        w1g = din("w1g", [D, DFF]); w1u = din("w1u", [D, DFF]); w1d = din("w1d", [DFF, D])
        w2g = din("w2g", [D, DFF]); w2u = din("w2u", [D, DFF]); w2d = din("w2d", [DFF, D])
    win = din("win", [D, INW]); wbf = din("wbf", [1024, D]); wbs = din("wbs", [1024, D]); wo = din("wo", [D, D])
    cmat = din("cmat", [128, 4, 128], BF16)
    cones = din("cones", [128, 128], F32)
    cdiag = din("cdiag", [128, 8, 512], BF16)
    cisel = din("cisel", [128, 8, 128], BF16)
    ckmask = din("ckmask", [2, LSEQ], BF16)
    cqsel = din("cqsel", [2, TO], BF16)
    cpresel = din("cpresel", [8, 2, 9], F32)
    outT = nc.dram_tensor("outT", [D, TO], F32, kind="ExternalOutput")
    hspill = nc.dram_tensor("hspill", [D, TO], F32)

    arena_cm = nc.sbuf_tensor("arena", [128, ARENA_BYTES // 2], BF16)
    arena_t = arena_cm.__enter__()
    AR = Arena(arena_t, ARENA_BYTES)
    ps_cms = [nc.psum_tensor("ps%d" % i, [128, 512], F32) for i in range(8)]
    PS = [cm.__enter__() for cm in ps_cms]
    PSR = [K.res("ps%d" % i) for i in range(8)]

    o = 0
    def alloc(shape, dt, parts=128):
        nonlocal o
        esz = 4 if dt == F32 else 2
        a = AR.carve(o, shape, dt, parts)
        o += (int(np.prod(shape)) * esz + 3) // 4 * 4
        return a
    ones_f = alloc([128], F32)
    cm_sb = alloc([4, 128], BF16)
    ones_b, ident_b, tri_b, tric_b = cm_sb[:, 0, :], cm_sb[:, 1, :], cm_sb[:, 2, :], cm_sb[:, 3, :]
    g1_sb = alloc([DC], F32); g2_sb = alloc([DC], F32); g3_sb = alloc([DC], F32)
    gq_sb = alloc([8], F32); gk_sb = alloc([8], F32); bfo_sb = alloc([1], F32); nb_sb = alloc([1], F32); gqs_sb = alloc([8], F32)
    diag_sb = alloc([8, 512], BF16)
    isel_sb = alloc([8, 128], BF16)
    kmask_sb = alloc([LSEQ], BF16)
    qsel_sb = alloc([TO], BF16)
    presel_sb = alloc([2, 9], F32)
    metaKf = alloc([8, NM], BF16); metaKs = alloc([8, NM], BF16)
    metaVf = alloc([1024], BF16); metaVs = alloc([1024], BF16)
    sp_own = alloc([TA], F32)
    xn = alloc([DC, TA], BF16)
    G_END = o
    H_OFF = G_END
    h = AR.carve(H_OFF, [DC, TA], F32)
    S_OFF = H_OFF + DC * TA * 4
    CONST = K.res("const")
    XNS = {"tiles": [], "res": []}

    def xnr(t0, tn):
        return [r for (a, n_), r in zip(XNS["tiles"], XNS["res"]) if a < t0 + tn and t0 < a + n_]
    HR = [[K.res("h%d_%d" % (c, t)) for t in range(3)] for c in range(DC)]

    C_A = K.res("constA")
    K.dma("sp", ones_f, cones.ap(), writes=[C_A])
    K.dma("sp", g1_sb, g1.ap(), writes=[C_A])
    TILES3 = [(0, 352), (352, 352), (704, 336)]
    TILES2 = [(0, 512), (512, 512)]
    xT_v = xT.ap().rearrange("(c p) t -> p c t", p=128)
    HT = [K.res("ht%d" % i) for i in range(3)]
    for ti, (t0, tn) in enumerate(TILES3):
        K.dma("sp", h[:, :, t0:t0 + tn], xT_v[:, :, t0:t0 + tn], writes=[HT[ti]])
        for c in range(DC):
            HR[c][ti].w = HT[ti].w
    K.dma("sp", cm_sb, cmat.ap(), writes=[C_A])
    for sb, dr in ((g2_sb, g2), (g3_sb, g3), (gq_sb, gq), (gk_sb, gk)):
        K.dma("sp", sb, dr.ap(), writes=[C_A])
    K.dma("sp", bfo_sb[0:8, :], bfo.ap(), writes=[C_A])
    K.dma("sp", diag_sb, cdiag.ap(), writes=[CONST])
    K.dma("sp", isel_sb, cisel.ap(), writes=[CONST])
    K.op("dve", lambda: nc.vector.memset(kmask_sb[:, :], 0.0), writes=[CONST])
    K.op("dve", lambda: nc.vector.memset(qsel_sb[:, :], 0.0), writes=[CONST])
    K.dma("sp", kmask_sb[0:2, :], ckmask.ap(), writes=[CONST])
    K.dma("sp", qsel_sb[0:2, :], cqsel.ap(), writes=[CONST])
    K.dma("sp", presel_sb[0:8, :, :], cpresel.ap(), writes=[CONST])
    HALL = K.res("hall")

    def rmsnorm(gain_sb, tiles, soff):
        sq = [AR.carve(soff + i * 2048, [512], F32) for i in range(2)]
        lnv = AR.carve(soff + 4096, [512], F32)
        rstd = AR.carve(soff + 6144, [512], F32)
        SQ = [K.res("sq0"), K.res("sq1")]
        LNV = K.res("lnv"); RSTD = K.res("rstd")
        XNS["tiles"] = list(tiles)
        XNS["res"] = [K.res("xn%d" % i) for i in range(len(tiles))]
        for ti, (t0, tn) in enumerate(tiles):
            pb = 6 + (ti % 2)
            for c in range(DC):
                K.op("act", lambda: nc.scalar.activation(out=sq[c % 2][:, 0:tn], in_=h[:, c, t0:t0 + tn], func=AF.Square),
                     reads=[HR[c][ti]], writes=[SQ[c % 2]])
                K.op("pe", lambda: nc.tensor.matmul(PS[pb][:, 0:tn], ones_f, sq[c % 2][:, 0:tn], start=(c == 0), stop=(c == DC - 1)),
                     reads=[SQ[c % 2], C_A], writes=[PSR[pb]])
            K.op("act", lambda: nc.scalar.activation(out=lnv[:, 0:tn], in_=PS[pb][:, 0:tn], func=AF.Ln, scale=1.0 / D, bias=EPS),
                 reads=[PSR[pb]], writes=[LNV])
            K.op("act", lambda: nc.scalar.activation(out=rstd[:, 0:tn], in_=lnv[:, 0:tn], func=AF.Exp, scale=-0.5),
                 reads=[LNV], writes=[RSTD])
            for c in range(DC):
                K.op("dve", lambda: nc.vector.scalar_tensor_tensor(out=xn[:, c, t0:t0 + tn], in0=h[:, c, t0:t0 + tn], scalar=gain_sb[:, c:c + 1],
                                                                   in1=rstd[:, 0:tn], op0=ALU.mult, op1=ALU.mult),
                     reads=[HR[c][ti], RSTD, C_A], writes=[XNS["res"][ti]])

    def ffn(wg, wu, wd, tiles, soff):
        wA = [AR.carve(soff + i * 8192, [DC, 256], BF16) for i in range(2)]
        wB = [AR.carve(soff + 16384 + i * 8192, [DC, 256], BF16) for i in range(2)]
        wD = [AR.carve(soff + 32768 + i * 8192, [2, D], BF16) for i in range(2)]
        act = [AR.carve(soff + 49152 + i * 4160, [2, TA], BF16) for i in range(2)]
        sg = [AR.carve(soff + 57472 + i * 2048, [512], F32) for i in range(2)]
        WA = [K.res("wA0"), K.res("wA1")]; WB = [K.res("wB0"), K.res("wB1")]; WD = [K.res("wD0"), K.res("wD1")]
        ACT = [K.res("act0"), K.res("act1")]; SG = [K.res("sg0"), K.res("sg1")]
        wg_v = wg.ap().rearrange("(c p) f -> p c f", p=128)
        wu_v = wu.ap().rearrange("(c p) f -> p c f", p=128)

        def load_ab(g):
            s = g % 2
            K.dma("pool", wA[s], wg_v[:, :, g * 256:(g + 1) * 256], writes=[WA[s]])
            K.dma("pool", wB[s], wu_v[:, :, g * 256:(g + 1) * 256], writes=[WB[s]])

        def load_d(g):
            s = g % 2
            K.dma("pool", wD[s], wd.ap()[g * 256:(g + 1) * 256, :].rearrange("(fc p) d -> p fc d", p=128), writes=[WD[s]])

        cnt = [0]

        def part_a(g):
            s = g % 2
            for fc in range(2):
                for ti, (t0, tn) in enumerate(tiles):
                    i = cnt[0] % 2
                    cnt[0] += 1
                    pg, pu = i, 2 + i
                    for c in range(DC):
                        K.op("pe", lambda: nc.tensor.matmul(PS[pg][:, 0:tn], wA[s][:, c, fc * 128:(fc + 1) * 128], xn[:, c, t0:t0 + tn], start=(c == 0), stop=(c == DC - 1)),
                             reads=[WA[s]] + xnr(t0, tn), writes=[PSR[pg]], mark=(c == DC - 1))
                    yield
                    for c in range(DC):
                        K.op("pe", lambda: nc.tensor.matmul(PS[pu][:, 0:tn], wB[s][:, c, fc * 128:(fc + 1) * 128], xn[:, c, t0:t0 + tn], start=(c == 0), stop=(c == DC - 1)),
                             reads=[WB[s]] + xnr(t0, tn), writes=[PSR[pu]], mark=(c == DC - 1))
                    K.op("act", lambda: nc.scalar.activation(out=sg[i][:, 0:tn], in_=PS[pg][:, 0:tn], func=AF.Silu), reads=[PSR[pg]], writes=[SG[i]])
                    K.op("dve", lambda: nc.vector.tensor_tensor(out=act[s][:, fc, t0:t0 + tn], in0=sg[i][:, 0:tn], in1=PS[pu][:, 0:tn], op=ALU.mult),
                         reads=[SG[i], PSR[pu]], writes=[ACT[s]])
                    yield

        dcnt = [0]

        def part_b(g):
            s = g % 2
            for dc in range(DC):
                for ti, (t0, tn) in enumerate(tiles):
                    pd = 4 + dcnt[0] % 4
                    dcnt[0] += 1
                    for fc in range(2):
                        K.op("pe", lambda: nc.tensor.matmul(PS[pd][:, 0:tn], wD[s][:, fc, dc * 128:(dc + 1) * 128], act[s][:, fc, t0:t0 + tn], start=(fc == 0), stop=(fc == 1)),
                             reads=[WD[s], ACT[s]], writes=[PSR[pd]], mark=(fc == 1))
                    K.op("dve", lambda: nc.vector.scalar_tensor_tensor(out=h[:, dc, t0:t0 + tn], in0=PS[pd][:, 0:tn], scalar=0.5, in1=h[:, dc, t0:t0 + tn], op0=ALU.mult, op1=ALU.add),
                         reads=[PSR[pd]], writes=[HR[dc][ti]])
                    yield

        def run(gen):
            for _ in gen:
                pass

        load_ab(0); load_d(0); load_ab(1); load_d(1)
        run(part_a(0))
        for g in range(NG):
            gb = part_b(g)
            if g + 1 < NG:
                for _ in part_a(g + 1):
                    for _k in range(4):
                        next(gb, None)
            run(gb)
            if g + 2 < NG:
                load_ab(g + 2)
                load_d(g + 2)

    QSCALE = 128.0 ** -0.5
    K.op("dve", lambda: nc.vector.tensor_scalar(out=nb_sb[0:8, :], in0=bfo_sb[0:8, :], scalar1=-1.0, scalar2=None, op0=ALU.mult),
         reads=[C_A], writes=[CONST])
    K.op("dve", lambda: nc.vector.tensor_scalar(out=gqs_sb, in0=gq_sb, scalar1=QSCALE, scalar2=None, op0=ALU.mult),
         reads=[C_A], writes=[CONST])

    if FULL:
        rmsnorm(g1_sb, TILES3, S_OFF + 61568)
        ffn(w1g, w1u, w1d, TILES3, S_OFF)
        K.barrier()

    rmsnorm(g2_sb, TILES3, S_OFF + 61504)
    HSP = K.res("hspill")
    hsp_v = hspill.ap().rearrange("(c p) t -> p c t", p=128)
    for c4 in range(4):
        K.dma("sp", hsp_v[:, 4 * c4:4 * c4 + 4, :], h[:, 4 * c4:4 * c4 + 4, 0:TO],
              reads=[HR[c][t] for c in range(4 * c4, 4 * c4 + 4) for t in range(3)], writes=[HSP])
    K.barrier()

    qf = AR.carve(H_OFF, [8, TO], BF16); qs = AR.carve(H_OFF + 16384, [8, TO], BF16)
    of = AR.carve(H_OFF + 32768, [8, TO], BF16); osb = AR.carve(H_OFF + 49152, [8, TO], BF16)
    QF = K.res("qf"); QS = K.res("qs"); OF = K.res("of"); OS = K.res("os")
    METAR = K.res("meta")
    SPO = K.res("spown")

    agin_k = {t: [nc.dram_tensor("agin_k%s%d" % (t, p), [1024, 512], BF16) for p in range(2)] for t in "fs"}
    agin_v = {t: [nc.dram_tensor("agin_v%s%d" % (t, p), [512, 1024], BF16) for p in range(2)] for t in "fs"}
    agout_k = {t: [nc.dram_tensor("agout_k%s%d" % (t, p), [4 * 1024, 512], BF16) for p in range(2)] for t in "fs"}
    agout_v = {t: [nc.dram_tensor("agout_v%s%d" % (t, p), [4 * 512, 1024], BF16) for p in range(2)] for t in "fs"}
    agin_f = nc.dram_tensor("agin_f", [8, TO], F32)
    agout_f = nc.dram_tensor("agout_f", [32, TO], F32)
    AGIN_K = {t: [K.res("agink%s%d" % (t, p)) for p in range(2)] for t in "fs"}
    AGIN_V = {t: [K.res("aginv%s%d" % (t, p)) for p in range(2)] for t in "fs"}
    AGOUT_K = {t: [K.res("agoutk%s%d" % (t, p)) for p in range(2)] for t in "fs"}
    AGOUT_V = {t: [K.res("agoutv%s%d" % (t, p)) for p in range(2)] for t in "fs"}
    AGIN_F = K.res("aginf"); AGOUT_F = K.res("agoutf")

    win_v = win.ap().rearrange("(c p) f -> p c f", p=128)

    def phase2():
        so = S_OFF
        wA = [AR.carve(so + i * 8192, [DC, 256], BF16) for i in range(2)]
        stg = [AR.carve(so + 16384 + i * 16384, [DC, 256], F32) for i in range(2)]
        kst = [AR.carve(so + 49152 + i * 2080, [TA], BF16) for i in range(2)]
        vst = [AR.carve(so + 53312 + i * 4096, [8, 256], BF16) for i in range(2)]
        sqb = [AR.carve(so + 61504 + i * 2048, [512], F32) for i in range(2)]
        lnv = AR.carve(so + 65600, [512], F32)
        rstd = AR.carve(so + 67648, [512], F32)
        fE = AR.carve(so + 69696, [TA], F32)
        WA = [K.res("p2wA%d" % i) for i in range(2)]
        STG = [K.res("p2stg%d" % i) for i in range(2)]
        KST = [K.res("kst0"), K.res("kst1")]; VST = [K.res("vst0"), K.res("vst1")]
        SQB = [K.res("sqb0"), K.res("sqb1")]; LNV = K.res("p2lnv"); RSTD = K.res("p2rstd"); FE = K.res("fE")

        groups = [("f", C_F, 0, 8)]
        for i in range(4):
            groups.append(("kf", C_FK + 256 * i, i, 256))
        for i in range(4):
            groups.append(("vf", C_FV + 256 * i, i, 256))
        for i in range(4):
            groups.append(("ks", C_SK + 256 * i, i, 256))
        for i in range(4):
            groups.append(("vs", C_SV + 256 * i, i, 256))
        n_kv = len(groups)
        for i in range(4):
            groups.append(("qf", C_FQ + 256 * i, i, 256))
        for i in range(4):
            groups.append(("qs", C_SQ + 256 * i, i, 256))

        def load(gi):
            kind, c0, i, ncol = groups[gi]
            K.dma("sp", stg[gi % 2][:, :, 0:ncol], win_v[:, :, c0:c0 + ncol], writes=[STG[gi % 2]])

        def convert(gi):
            kind, c0, i, ncol = groups[gi]
            K.op("dve", lambda: nc.vector.tensor_copy(out=wA[gi % 2][:, :, 0:ncol], in_=stg[gi % 2][:, :, 0:ncol]), reads=[STG[gi % 2]], writes=[WA[gi % 2]])

        rot = [0]
        hcnt = [0]
        vcnt = [0]

        def normed(pk, pidx, tn, gain_ap, dest, dres):
            i2 = rot[0] % 2
            K.op("act", lambda: nc.scalar.activation(out=sqb[i2][:, 0:tn], in_=PS[pidx][:, 0:tn], func=AF.Square), reads=[PSR[pidx]], writes=[SQB[i2]])
            K.op("pe", lambda: nc.tensor.matmul(PS[6 + i2][:, 0:tn], ones_f, sqb[i2][:, 0:tn], start=True, stop=True), reads=[SQB[i2], CONST], writes=[PSR[6 + i2]])
            K.op("act", lambda: nc.scalar.activation(out=lnv[:, 0:tn], in_=PS[6 + i2][:, 0:tn], func=AF.Ln, scale=1.0 / 128, bias=EPS), reads=[PSR[6 + i2]], writes=[LNV])
            K.op("act", lambda: nc.scalar.activation(out=rstd[:, 0:tn], in_=lnv[:, 0:tn], func=AF.Exp, scale=-0.5), reads=[LNV], writes=[RSTD])
            K.op("dve", lambda: nc.vector.scalar_tensor_tensor(out=dest, in0=PS[pidx][:, 0:tn], scalar=gain_ap, in1=rstd[:, 0:tn], op0=ALU.mult, op1=ALU.mult),
                 reads=[PSR[pidx], RSTD, CONST], writes=[dres])

        def compute(gi):
            kind, c0, i, ncol = groups[gi]
            s = gi % 2
            if kind == "f":
                for ti, (t0, tn) in enumerate(TILES3):
                    for c in range(DC):
                        K.op("pe", lambda: nc.tensor.matmul(PS[4][0:8, 0:tn], wA[s][:, c, 0:8], xn[:, c, t0:t0 + tn], start=(c == 0), stop=(c == DC - 1)),
                             reads=[WA[s]] + xnr(t0, tn), writes=[PSR[4]], mark=(c == DC - 1))
                    K.op("act", lambda: nc.scalar.activation(out=fE[0:8, t0:t0 + tn], in_=PS[4][0:8, 0:tn], func=AF.Exp, scale=-1.0, bias=nb_sb[0:8, :]),
                         reads=[PSR[4], CONST], writes=[FE])
                    K.op("act", lambda: nc.scalar.activation(out=sp_own[0:8, t0:t0 + tn], in_=fE[0:8, t0:t0 + tn], func=AF.Ln, bias=1.0),
                         reads=[FE], writes=[SPO])
                K.dma("pool", agin_f.ap(), sp_own[0:8, 0:TO], reads=[SPO], writes=[AGIN_F])
            elif kind in ("kf", "ks", "qf", "qs"):
                typ = kind[1]
                isq = kind[0] == "q"
                tiles = TILES2 if isq else TILES3
                for hh in range(2):
                    head = 2 * i + hh
                    hb = hcnt[0] % 2
                    hcnt[0] += 1
                    for ti, (t0, tn) in enumerate(tiles):
                        pidx = rot[0] % 4
                        rot[0] += 1
                        for c in range(DC):
                            K.op("pe", lambda: nc.tensor.matmul(PS[pidx][:, 0:tn], wA[s][:, c, hh * 128:(hh + 1) * 128], xn[:, c, t0:t0 + tn], start=(c == 0), stop=(c == DC - 1)),
                                 reads=[WA[s]] + xnr(t0, tn), writes=[PSR[pidx]], mark=(c == DC - 1))
                        if isq:
                            dest = (qf if typ == "f" else qs)[:, head, t0:t0 + tn]
                            dres = QF if typ == "f" else QS
                        else:
                            dest = kst[hb][:, t0:t0 + tn]
                            dres = KST[hb]
                        if typ == "f":
                            gain_ap = (gqs_sb if isq else gk_sb)[:, head:head + 1]
                            normed(PS[pidx], pidx, tn, gain_ap, dest, dres)
                        elif isq:
                            K.op("dve", lambda: nc.vector.tensor_scalar(out=dest, in0=PS[pidx][:, 0:tn], scalar1=QSCALE, scalar2=None, op0=ALU.mult),
                                 reads=[PSR[pidx]], writes=[dres])
                        else:
                            K.op("dve", lambda: nc.vector.tensor_copy(out=dest, in_=PS[pidx][:, 0:tn]), reads=[PSR[pidx]], writes=[dres])
                    if not isq:
                        mk = metaKf if typ == "f" else metaKs
                        K.op("dve", lambda: nc.vector.tensor_copy(out=mk[:, head, :], in_=kst[hb][:, TO:TA]), reads=[KST[hb]], writes=[METAR])
                        for pos in range(2):
                            K.dma("pool", agin_k[typ][pos].ap()[head * 128:(head + 1) * 128, :], kst[hb][:, pos * 512:(pos + 1) * 512],
                                  reads=[KST[hb]], writes=[AGIN_K[typ][pos]])
            else:
                typ = kind[1]
                vb = vcnt[0] % 2
                vcnt[0] += 1
                mv = metaVf if typ == "f" else metaVs
                for tb in range(9):
                    M = 128 if tb < 8 else NM
                    pidx = rot[0] % 4
                    rot[0] += 1
                    for c in range(DC):
                        K.op("pe", lambda: nc.tensor.matmul(PS[pidx][0:M, 0:256], xn[:, c, tb * 128:tb * 128 + M], wA[s][:, c, 0:256], start=(c == 0), stop=(c == DC - 1)),
                             reads=[WA[s]] + xnr(tb * 128, M), writes=[PSR[pidx]], mark=(c == DC - 1))
                    if tb < 8:
                        K.op("act", lambda: nc.scalar.copy(out=vst[vb][:, tb, :], in_=PS[pidx][:, 0:256]), reads=[PSR[pidx]], writes=[VST[vb]])
                    else:
                        K.op("act", lambda: nc.scalar.copy(out=mv[0:NM, i * 256:(i + 1) * 256], in_=PS[pidx][0:NM, 0:256]), reads=[PSR[pidx]], writes=[METAR])
                for pos in range(2):
                    K.dma("pool", agin_v[typ][pos].ap()[:, i * 256:(i + 1) * 256].rearrange("(tb p) f -> p tb f", p=128), vst[vb][:, 4 * pos:4 * pos + 4, :],
                          reads=[VST[vb]], writes=[AGIN_V[typ][pos]])

        def collectives(gi):
            kind, c0, i, ncol = groups[gi]
            if kind == "f":
                K.collective(agin_f.ap().opt(), agout_f.ap().opt(), [AGIN_F], [AGOUT_F], GROUPS)
            elif i == 3 and kind[0] in "kv":
                typ = kind[1]
                for pos in range(2):
                    if kind[0] == "k":
                        K.collective(agin_k[typ][pos].ap().opt(), agout_k[typ][pos].ap().opt(), [AGIN_K[typ][pos]], [AGOUT_K[typ][pos]], GROUPS)
                    else:
                        K.collective(agin_v[typ][pos].ap().opt(), agout_v[typ][pos].ap().opt(), [AGIN_V[typ][pos]], [AGOUT_V[typ][pos]], GROUPS)

        load(0); load(1); convert(0)
        for gi in range(len(groups)):
            if gi + 1 < len(groups):
                convert(gi + 1)
            compute(gi)
            if gi + 2 < len(groups):
                load(gi + 2)
            collectives(gi)

    phase2()
    K.barrier(skip_cc=True)

    cs_k = nc.dram_tensor("cs_k", [3, 8, LSEQ], BF16)
    cs_q = nc.dram_tensor("cs_q", [3, 8, TO], BF16)
    CSK = K.res("csk"); CSQ = K.res("csq")

    def chunk_src(ch):
        return (ch, 0) if ch < 4 else (7 - ch, 1)

    def phase25():
        so = S_OFF
        spg = AR.carve(so, [LSEQ], F32)
        Cg = AR.carve(so + 16448, [LSEQ], F32)
        kp = [AR.carve(so + 32896 + i * 8224, [LSEQ], BF16) for i in range(3)]
        Cown = AR.carve(so + 57568, [TO], F32)
        qp = [AR.carve(so + 61664 + i * 2048, [TO], BF16) for i in range(3)]
        sbv = AR.carve(so + 67808, [2], F32)
        SPG = K.res("spg"); CG = K.res("cg"); KP = K.res("kp"); COWN = K.res("cown"); QP = K.res("qp"); SBV = K.res("sbv")
        K.dma("sp", spg[0:8, 0:NM], sp_own[0:8, TO:TA], reads=[SPO], writes=[SPG])
        for ch in range(8):
            rank, pos = chunk_src(ch)
            K.dma("sp", spg[0:8, NM + ch * 512:NM + (ch + 1) * 512], agout_f.ap()[rank * 8:rank * 8 + 8, pos * 512:(pos + 1) * 512],
                  reads=[AGOUT_F], writes=[SPG])
        K.op("dve", lambda: nc.vector.tensor_tensor_scan(out=Cg[0:8, :], data0=spg[0:8, :], data1=spg[0:8, :], initial=0.0, op0=ALU.add, op1=ALU.bypass),
             reads=[SPG], writes=[CG])
        K.op("dve", lambda: nc.vector.memset(sbv[0:8, :], 0.0), writes=[SBV])
        for qi in range(2):
            for k in range(9):
                pos = NM - 1 + 512 * k
                K.op("dve", lambda: nc.vector.scalar_tensor_tensor(out=sbv[0:8, qi:qi + 1], in0=Cg[0:8, pos:pos + 1], scalar=presel_sb[0:8, qi, k:k + 1],
                                                                   in1=sbv[0:8, qi:qi + 1], op0=ALU.mult, op1=ALU.add),
                     reads=[CG, CONST, SBV], writes=[SBV])
        for qi in range(2):
            K.op("dve", lambda: nc.vector.tensor_tensor_scan(out=Cown[0:8, qi * 512:(qi + 1) * 512], data0=sp_own[0:8, qi * 512:(qi + 1) * 512],
                                                            data1=sp_own[0:8, qi * 512:(qi + 1) * 512], initial=sbv[0:8, qi:qi + 1], op0=ALU.add, op1=ALU.bypass),
                 reads=[SPO, SBV], writes=[COWN])
        K.op("dve", lambda: nc.vector.tensor_copy(out=kp[0][0:8, :], in_=Cg[0:8, :]), reads=[CG], writes=[KP])
        K.op("dve", lambda: nc.vector.tensor_tensor(out=spg[0:8, :], in0=Cg[0:8, :], in1=kp[0][0:8, :], op=ALU.subtract), reads=[CG, KP], writes=[SPG])
        K.op("dve", lambda: nc.vector.tensor_copy(out=kp[1][0:8, :], in_=spg[0:8, :]), reads=[SPG], writes=[KP])
        K.op("dve", lambda: nc.vector.tensor_tensor(out=spg[0:8, :], in0=spg[0:8, :], in1=kp[1][0:8, :], op=ALU.subtract), reads=[SPG, KP], writes=[SPG])
        K.op("dve", lambda: nc.vector.tensor_copy(out=kp[2][0:8, :], in_=spg[0:8, :]), reads=[SPG], writes=[KP])
        K.op("dve", lambda: nc.vector.tensor_scalar(out=Cown[0:8, :], in0=Cown[0:8, :], scalar1=-1.0, scalar2=None, op0=ALU.mult), reads=[COWN], writes=[COWN])
        K.op("dve", lambda: nc.vector.tensor_copy(out=qp[0][0:8, :], in_=Cown[0:8, :]), reads=[COWN], writes=[QP])
        K.op("dve", lambda: nc.vector.tensor_tensor(out=Cown[0:8, :], in0=Cown[0:8, :], in1=qp[0][0:8, :], op=ALU.subtract), reads=[COWN, QP], writes=[COWN])
        K.op("dve", lambda: nc.vector.tensor_copy(out=qp[1][0:8, :], in_=Cown[0:8, :]), reads=[COWN], writes=[QP])
        K.op("dve", lambda: nc.vector.tensor_tensor(out=Cown[0:8, :], in0=Cown[0:8, :], in1=qp[1][0:8, :], op=ALU.subtract), reads=[COWN, QP], writes=[COWN])
        K.op("dve", lambda: nc.vector.tensor_copy(out=qp[2][0:8, :], in_=Cown[0:8, :]), reads=[COWN], writes=[QP])
        for i in range(3):
            K.dma("sp", cs_k.ap()[i], kp[i][0:8, :], reads=[KP], writes=[CSK])
            K.dma("sp", cs_q.ap()[i], qp[i][0:8, :], reads=[QP], writes=[CSQ])

    phase25()
    K.barrier(skip_cc=True)

    def phase3():
        so = S_OFF
        kt = [AR.carve(so + i * 8192, [8, 512], BF16) for i in range(2)]
        vv = [AR.carve(so + 16384 + i * 8192, [32, 128], BF16) for i in range(2)]
        kaug = [AR.carve(so + 32768 + i * 8224, [LSEQ], BF16) for i in range(2)]
        qaug = [AR.carve(so + 49216 + i * 2048, [TO], BF16) for i in range(2)]
        Pb = [AR.carve(so + 53312 + i * 1024, [512], BF16) for i in range(2)]
        Eb = [AR.carve(so + 55360 + i * 2048, [512], F32) for i in range(2)]
        Lb = [AR.carve(so + 59456 + i * 1024, [512], BF16) for i in range(2)]
        Tb = [AR.carve(so + 61504 + i * 2048, [512], F32) for i in range(2)]
        Xb = [AR.carve(so + 65600 + i * 2048, [512], F32) for i in range(2)]
        Wb = [AR.carve(so + 69696 + i * 1024, [512], BF16) for i in range(2)]
        rden = AR.carve(so + 71744, [512], F32)
        KTC = [[K.res("kt%d_%d" % (t, c)) for c in range(8)] for t in range(2)]
        VVC = [[K.res("vv%d_%d" % (t, c)) for c in range(8)] for t in range(2)]
        KAUG = [K.res("kaug0"), K.res("kaug1")]; QAUG = [K.res("qaug0"), K.res("qaug1")]
        PB = [K.res("pb0"), K.res("pb1")]; EB = [K.res("eb0"), K.res("eb1")]; LB = [K.res("lb0"), K.res("lb1")]
        TB = [K.res("tb0"), K.res("tb1")]; XB = [K.res("xb0"), K.res("xb1")]; WB_ = [K.res("wb0"), K.res("wb1")]
        RDEN = K.res("rden")
        for i in range(2):
            K.op("dve", lambda: nc.vector.memset(kaug[i][:, :], 0.0), writes=[KAUG[i]])
            K.op("dve", lambda: nc.vector.memset(qaug[i][:, :], 0.0), writes=[QAUG[i]])
            K.op("dve", lambda: nc.vector.memset(kaug[i][0:8, :], 1.0), writes=[KAUG[i]])
            K.op("dve", lambda: nc.vector.memset(qaug[i][0:8, :], 1.0), writes=[QAUG[i]])
            K.dma("sp", kaug[i][6:8, :], ckmask.ap(), writes=[KAUG[i]])
            K.dma("sp", qaug[i][6:8, :], cqsel.ap(), writes=[QAUG[i]])

        def load_chunk(t, typ, h, ch):
            rank, pos = chunk_src(ch)
            K.dma("sp", kt[t][:, ch, :], agout_k[typ][pos].ap()[rank * 1024 + h * 128:rank * 1024 + (h + 1) * 128, :],
                  reads=[AGOUT_K[typ][pos]], writes=[KTC[t][ch]])
            K.dma("sp", vv[t][:, 4 * ch:4 * ch + 4, :],
                  agout_v[typ][pos].ap()[rank * 512:(rank + 1) * 512, h * 128:(h + 1) * 128].rearrange("(kb p) d -> p kb d", p=128),
                  reads=[AGOUT_V[typ][pos]], writes=[VVC[t][ch]])

        def load_pair(h):
            ab = h % 2
            K.dma("sp", kaug[ab][0:3, :], cs_k.ap()[:, h, :], reads=[CSK], writes=[KAUG[ab]])
            K.dma("sp", qaug[ab][3:6, :], cs_q.ap()[:, h, :], reads=[CSQ], writes=[QAUG[ab]])
            for c in range(8):
                load_chunk(0, "f", h, c)
                load_chunk(1, "s", h, [3, 2, 1, 0, 7, 6, 5, 4][c] if h == 0 else 7 - c)

        def blocks_for(qt):
            chunks = range(4) if qt == 0 else range(8)
            bl = [("meta", 0, 0)]
            for ch in chunks:
                for kb in range(4):
                    bl.append(("blk", ch, kb))
            return bl

        def unit_of(qt, ch):
            if qt == 0:
                return ch
            return ch if ch >= 4 else None

        def blk_aps(t, h, b):
            kind, ch, kb = b
            if kind == "meta":
                mk = metaKf if t == 0 else metaKs
                mv = metaVf if t == 0 else metaVs
                return NM, mk[:, h, :], mv[0:NM, h * 128:(h + 1) * 128], 0, [METAR], [METAR]
            return (128, kt[t][:, ch, kb * 128:(kb + 1) * 128], vv[t][:, 4 * ch + kb, :], NM + ch * 512 + kb * 128,
                    [KTC[t][ch]], [VVC[t][ch]])

        def pair(h):
            ab = h % 2
            for qt in range(2):
                q0 = qt * 512
                blf = blocks_for(qt)
                bls = list(reversed(blf[1:])) + [blf[0]]
                n = len(blf)

                def s_block(i):
                    kind, ch, kb = blf[i]
                    nk, kT, vB, col, kres, vres = blk_aps(0, h, blf[i])
                    si = i % 2
                    u = unit_of(qt, ch) if kind == "blk" else None
                    K.op("pe", lambda: nc.tensor.matmul(PS[si][0:nk, :], kT, qf[:, h, q0:q0 + 512], start=True, stop=False),
                         reads=kres + [QF], writes=[PSR[si]], mark=False)
                    K.op("pe", lambda: nc.tensor.matmul(PS[si][0:nk, :], kaug[ab][:, col:col + nk], qaug[ab][:, q0:q0 + 512], start=False, stop=(u is None)),
                         reads=[KAUG[ab], QAUG[ab], QF] + kres, writes=[PSR[si]], mark=(u is None))
                    if u is not None:
                        K.op("pe", lambda: nc.tensor.matmul(PS[si][0:nk, :], isel_sb[:, u, :], diag_sb[:, kb, :], start=False, stop=True),
                             reads=[CONST, KAUG[ab], QAUG[ab], QF] + kres, writes=[PSR[si]])
                    K.op("act", lambda: nc.scalar.activation(out=Pb[si][0:nk, :], in_=PS[si][0:nk, :], func=AF.Exp), reads=[PSR[si]], writes=[PB[si]])

                def pv_block(i):
                    nk, kT, vB, col, kres, vres = blk_aps(0, h, blf[i])
                    si = i % 2
                    K.op("pe", lambda: nc.tensor.matmul(PS[2][:, :], vB, Pb[si][0:nk, :], start=(i == 0), stop=(i == n - 1)),
                         reads=vres + [PB[si]], writes=[PSR[2]], mark=False)
                    K.op("pe", lambda: nc.tensor.matmul(PS[3][:, :], ones_b[0:nk, :], Pb[si][0:nk, :], start=(i == 0), stop=(i == n - 1)),
                         reads=[CONST, PB[si]] + vres, writes=[PSR[3], PSR[2]])

                def z_block(i):
                    kind, ch, kb = bls[i]
                    nk, kT, vB, col, kres, vres = blk_aps(1, h, bls[i])
                    zi = 4 + i % 2
                    j = i % 2
                    u = unit_of(qt, ch) if kind == "blk" else None
                    K.op("pe", lambda: nc.tensor.matmul(PS[zi][0:nk, :], kT, qs[:, h, q0:q0 + 512], start=True, stop=(u is None)),
                         reads=kres + [QS], writes=[PSR[zi]], mark=(u is None))
                    if u is not None:
                        K.op("pe", lambda: nc.tensor.matmul(PS[zi][0:nk, :], kmask_sb[:, col:col + nk], qsel_sb[:, q0:q0 + 512], start=False, stop=False),
                             reads=[CONST], writes=[PSR[zi]], mark=False)
                        K.op("pe", lambda: nc.tensor.matmul(PS[zi][0:nk, :], isel_sb[:, u, :], diag_sb[:, 4 + kb, :], start=False, stop=True),
                             reads=[CONST, QS] + kres, writes=[PSR[zi]])
                    K.op("act", lambda: nc.scalar.activation(out=Eb[j][0:nk, :], in_=PS[zi][0:nk, :], func=AF.Exp), reads=[PSR[zi]], writes=[EB[j]])
                    K.op("act", lambda: nc.scalar.activation(out=Lb[j][0:nk, :], in_=Eb[j][0:nk, :], func=AF.Ln, bias=1.0), reads=[EB[j]], writes=[LB[j]])

                def tril(i):
                    nk = NM if bls[i][0] == "meta" else 128
                    j = i % 2
                    K.op("pe", lambda: nc.tensor.matmul(PS[6][0:nk, :], tri_b[0:nk, 0:nk], Lb[j][0:nk, :], start=(i == 0), stop=(i == n - 1)),
                         reads=[LB[j], CONST], writes=[PSR[6]])
                    K.op("act", lambda: nc.scalar.activation(out=Xb[j][0:nk, :], in_=PS[6][0:nk, :], func=AF.Exp, scale=-1.0), reads=[PSR[6]], writes=[XB[j]])

                def tric_w(i):
                    nk = NM if bls[i][0] == "meta" else 128
                    j = i % 2
                    if i < n - 1:
                        K.op("pe", lambda: nc.tensor.matmul(PS[6][:, :], tric_b[0:nk, :], Lb[j][0:nk, :], start=False, stop=False),
                             reads=[LB[j], CONST], writes=[PSR[6]])
                    K.op("dve", lambda: nc.vector.tensor_tensor(out=Wb[j][0:nk, :], in0=Eb[j][0:nk, :], in1=Xb[j][0:nk, :], op=ALU.mult),
                         reads=[EB[j], XB[j]], writes=[WB_[j]])

                def pv_s(i):
                    nk, kT, vB, col, kres, vres = blk_aps(1, h, bls[i])
                    j = i % 2
                    K.op("pe", lambda: nc.tensor.matmul(PS[7][:, :], vB, Wb[j][0:nk, :], start=(i == 0), stop=(i == n - 1)),
                         reads=vres + [WB_[j]], writes=[PSR[7]])

                s_block(0)
                z_block(0)
                for i in range(n):
                    tril(i)
                    if i + 1 < n:
                        s_block(i + 1)
                        z_block(i + 1)
                    pv_block(i)
                    if i > 0:
                        pv_s(i - 1)
                    tric_w(i)
                pv_s(n - 1)
                K.op("dve", lambda: nc.vector.reciprocal(out=rden, in_=PS[3][:, :]), reads=[PSR[3]], writes=[RDEN])
                K.op("dve", lambda: nc.vector.tensor_tensor(out=of[:, h, q0:q0 + 512], in0=PS[2][:, :], in1=rden, op=ALU.mult),
                     reads=[PSR[2], PSR[3], RDEN], writes=[OF])
                K.op("dve", lambda: nc.vector.tensor_copy(out=osb[:, h, q0:q0 + 512], in_=PS[7][:, :]),
                     reads=[PSR[7]], writes=[OS])

        load_pair(0)
        for h in range(8):
            pair(h)
            if h + 1 < 8:
                load_pair(h + 1)

    phase3()
    K.barrier()

    merged = AR.carve(S_OFF, [DC, TO], BF16)
    MERGED = K.res("merged")
    wbf_v = wbf.ap().rearrange("(kc p) d -> p kc d", p=128)
    wbs_v = wbs.ap().rearrange("(kc p) d -> p kc d", p=128)

    def phase4a():
        so = S_OFF + 32768
        wgf = [AR.carve(so + i * 4096, [DC, 128], BF16) for i in range(2)]
        wgs = [AR.carve(so + 8192 + i * 4096, [DC, 128], BF16) for i in range(2)]
        wbfb = [AR.carve(so + 16384 + i * 2048, [8, 128], BF16) for i in range(2)]
        wbsb = [AR.carve(so + 20480 + i * 2048, [8, 128], BF16) for i in range(2)]
        sgf = [AR.carve(so + 24576 + i * 2048, [512], F32) for i in range(2)]
        sgs = [AR.carve(so + 28672 + i * 2048, [512], F32) for i in range(2)]
        W4 = [K.res("w4_0"), K.res("w4_1")]
        SGF = [K.res("sgf0"), K.res("sgf1")]; SGS = [K.res("sgs0"), K.res("sgs1")]

        def load(m):
            s = m % 2
            K.dma("pool", wgf[s], win_v[:, :, C_GF + m * 128:C_GF + (m + 1) * 128], writes=[W4[s]])
            K.dma("pool", wgs[s], win_v[:, :, C_GS + m * 128:C_GS + (m + 1) * 128], writes=[W4[s]])
            K.dma("pool", wbfb[s], wbf_v[:, :, m * 128:(m + 1) * 128], writes=[W4[s]])
            K.dma("pool", wbsb[s], wbs_v[:, :, m * 128:(m + 1) * 128], writes=[W4[s]])

        load(0); load(1)
        cnt = 0
        for m in range(DC):
            s = m % 2
            for ti, (t0, tn) in enumerate(TILES2):
                pb = 4 * (cnt % 2)
                j = cnt % 2
                cnt += 1
                for kc in range(8):
                    K.op("pe", lambda: nc.tensor.matmul(PS[pb][:, :], wbfb[s][:, kc, :], of[:, kc, t0:t0 + tn], start=(kc == 0), stop=(kc == 7)),
                         reads=[W4[s], OF], writes=[PSR[pb]], mark=(kc == 7))
                for c in range(DC):
                    K.op("pe", lambda: nc.tensor.matmul(PS[pb + 1][:, :], wgf[s][:, c, :], xn[:, c, t0:t0 + tn], start=(c == 0), stop=(c == DC - 1)),
                         reads=[W4[s]] + xnr(t0, tn), writes=[PSR[pb + 1]], mark=(c == DC - 1))
                for kc in range(8):
                    K.op("pe", lambda: nc.tensor.matmul(PS[pb + 2][:, :], wbsb[s][:, kc, :], osb[:, kc, t0:t0 + tn], start=(kc == 0), stop=(kc == 7)),
                         reads=[W4[s], OS], writes=[PSR[pb + 2]], mark=(kc == 7))
                for c in range(DC):
                    K.op("pe", lambda: nc.tensor.matmul(PS[pb + 3][:, :], wgs[s][:, c, :], xn[:, c, t0:t0 + tn], start=(c == 0), stop=(c == DC - 1)),
                         reads=[W4[s]] + xnr(t0, tn), writes=[PSR[pb + 3]], mark=(c == DC - 1))
                K.op("act", lambda: nc.scalar.activation(out=sgf[j], in_=PS[pb + 1][:, :], func=AF.Sigmoid), reads=[PSR[pb + 1]], writes=[SGF[j]])
                K.op("act", lambda: nc.scalar.activation(out=sgs[j], in_=PS[pb + 3][:, :], func=AF.Sigmoid), reads=[PSR[pb + 3]], writes=[SGS[j]])
                K.op("dve", lambda: nc.vector.tensor_tensor(out=sgf[j], in0=sgf[j], in1=PS[pb][:, :], op=ALU.mult), reads=[SGF[j], PSR[pb]], writes=[SGF[j]])
                K.op("dve", lambda: nc.vector.tensor_tensor(out=sgs[j], in0=sgs[j], in1=PS[pb + 2][:, :], op=ALU.mult), reads=[SGS[j], PSR[pb + 2]], writes=[SGS[j]])
                K.op("dve", lambda: nc.vector.tensor_tensor(out=merged[:, m, t0:t0 + tn], in0=sgf[j], in1=sgs[j], op=ALU.add), reads=[SGF[j], SGS[j]], writes=[MERGED])
            if m + 2 < DC:
                load(m + 2)

    phase4a()
    K.barrier()
    for c4 in range(4):
        K.dma("sp", h[:, 4 * c4:4 * c4 + 4, 0:TO], hsp_v[:, 4 * c4:4 * c4 + 4, :], reads=[HSP], writes=[HALL])
    for c in range(DC):
        for t in range(2):
            HR[c][t].w = HALL.w
            HR[c][t].r = {}

    def phase4b():
        so = S_OFF + 32768
        wos = [AR.carve(so + i * 8192, [DC, 256], BF16) for i in range(2)]
        WO = [K.res("wo0"), K.res("wo1")]
        wo_v = wo.ap().rearrange("(c p) f -> p c f", p=128)

        def load(g):
            K.dma("pool", wos[g % 2], wo_v[:, :, g * 256:(g + 1) * 256], writes=[WO[g % 2]])

        load(0); load(1)
        cnt = 0
        for g in range(8):
            s = g % 2
            for mc in range(2):
                m2 = 2 * g + mc
                for ti, (t0, tn) in enumerate(TILES2):
                    pb = cnt % 4
                    cnt += 1
                    for m in range(DC):
                        K.op("pe", lambda: nc.tensor.matmul(PS[pb][:, :], wos[s][:, m, mc * 128:(mc + 1) * 128], merged[:, m, t0:t0 + tn], start=(m == 0), stop=(m == DC - 1)),
                             reads=[WO[s], MERGED], writes=[PSR[pb]], mark=(m == DC - 1))
                    K.op("dve", lambda: nc.vector.tensor_tensor(out=h[:, m2, t0:t0 + tn], in0=PS[pb][:, :], in1=h[:, m2, t0:t0 + tn], op=ALU.add),
                         reads=[PSR[pb], HR[m2][ti]], writes=[HR[m2][ti]])
            if g + 2 < 8:
                load(g + 2)

    phase4b()
    K.barrier()

    if FULL:
        rmsnorm(g3_sb, TILES2, S_OFF + 61568)
        ffn(w2g, w2u, w2d, TILES2, S_OFF)
        K.barrier()

    OUT = K.res("out")
    outT_v = outT.ap().rearrange("(c p) t -> p c t", p=128)
    for c4 in range(4):
        K.dma("sp", outT_v[:, 4 * c4:4 * c4 + 4, :], h[:, 4 * c4:4 * c4 + 4, 0:TO], writes=[OUT])
    K.wait("sp", OUT.w)
    return nc


def _bf(a):
    return np.ascontiguousarray(a).astype(ml_dtypes.bfloat16)


def make_consts():
    ones = np.ones((128, 128), np.float32)
    ident = np.eye(128, dtype=np.float32)
    sp = np.arange(128)[:, None]; s = np.arange(128)[None, :]
    tri = (sp >= s).astype(np.float32)
    tric = (sp < s).astype(np.float32)
    cmat = np.stack([ones, ident, tri, tric], axis=1)
    p = np.arange(128)[:, None]; f = np.arange(512)[None, :]
    diag = np.zeros((128, 8, 512), np.float32)
    for kb in range(4):
        diag[:, kb, :] = np.where(kb * 128 + p <= f, 0.0, NEG)
        diag[:, 4 + kb, :] = np.where(kb * 128 + p < f, 0.0, NEG)
    qsel = np.zeros((2, TO), np.float32)
    qsel[0, :512] = 1.0
    qsel[1, 512:] = 1.0
    return dict(cmat=_bf(cmat), cones=ones, cdiag=_bf(diag), cqsel=_bf(qsel))


def core_consts(j):
    A, B = j, 7 - j
    isel = np.zeros((128, 8, 128), np.float32)
    isel[:, A, :] = np.eye(128)
    isel[:, 4 + (B - 4), :] = np.eye(128)
    kmask = np.zeros((2, LSEQ), np.float32)
    kmask[0, NM + (A + 1) * 512:] = NEG
    kmask[1, NM + (B + 1) * 512:] = NEG
    presel = np.zeros((8, 2, 9), np.float32)
    presel[:, 0, A] = 1.0
    presel[:, 1, B] = 1.0
    return dict(cisel=_bf(isel), ckmask=_bf(kmask), cpresel=presel)


_NC_CACHE = {}


def kernel(x, meta_tokens, ffn1_norm, ffn1_w_gate, ffn1_w_up, ffn1_w_down, mix_norm, w_in, b_forget,
           fox_q_norm, fox_k_norm, w_branch_fox, w_branch_sb, w_out, ffn2_norm, ffn2_w_gate, ffn2_w_up,
           ffn2_w_down, _mode="full", _h1=None, _ncores=8):
    f32 = lambda a: np.ascontiguousarray(np.asarray(a, dtype=np.float32))
    x = f32(x); meta = f32(meta_tokens)
    gl = lambda g: f32(np.asarray(g)[0].reshape(DC, 128).T)
    common = dict(
        g1=gl(ffn1_norm), g2=gl(mix_norm), g3=gl(ffn2_norm),
        gq=f32(np.asarray(fox_q_norm)[0].T), gk=f32(np.asarray(fox_k_norm)[0].T),
        bfo=f32(np.asarray(b_forget)[0].reshape(8, 1)),
        w1g=f32(np.asarray(ffn1_w_gate)[0]), w1u=f32(np.asarray(ffn1_w_up)[0]), w1d=f32(np.asarray(ffn1_w_down)[0]),
        w2g=f32(np.asarray(ffn2_w_gate)[0]), w2u=f32(np.asarray(ffn2_w_up)[0]), w2d=f32(np.asarray(ffn2_w_down)[0]),
        win=f32(np.asarray(w_in)[0]), wbf=f32(np.asarray(w_branch_fox)[0]), wbs=f32(np.asarray(w_branch_sb)[0]),
        wo=f32(np.asarray(w_out)[0]),
    )
    common.update(make_consts())
    in_maps = []
    for core in range(_ncores):
        b, j = core // 4, core % 4
        A, B = j, 7 - j
        if _h1 is None:
            own = np.concatenate([x[b, A * 512:(A + 1) * 512], x[b, B * 512:(B + 1) * 512], meta], axis=0)
        else:
            hb = _h1[b]
            own = np.concatenate([hb[16 + A * 512:16 + (A + 1) * 512], hb[16 + B * 512:16 + (B + 1) * 512], hb[0:16]], axis=0)
        m = dict(common)
        m["xT"] = np.ascontiguousarray(own.T)
        m.update(core_consts(j))
        in_maps.append(m)
    key = (_mode, _ncores)
    if key not in _NC_CACHE:
        _NC_CACHE[key] = build(_mode, _ncores // 4)
    nc = _NC_CACHE[key]
    if _mode != "full":
        for m in in_maps:
            for k in ("w1g", "w1u", "w1d", "w2g", "w2u", "w2d"):
                m.pop(k, None)
    res = run_bass_kernel_spmd(nc, in_maps, core_ids=list(range(_ncores)))
    out = np.zeros((2, 4096, D), np.float32)
    for core in range(_ncores):
        b, j = core // 4, core % 4
        A, B = j, 7 - j
        o = np.asarray(res.results[core]["outT"]).T
        out[b, A * 512:(A + 1) * 512] = o[:512]
        out[b, B * 512:(B + 1) * 512] = o[512:]
    return out
```

```python
import numpy as np
import ml_dtypes
import concourse.bass as bass
import concourse.mybir as mybir
from concourse.bass_utils import run_bass_kernel_spmd

F32 = mybir.dt.float32
BF16 = mybir.dt.bfloat16
AF = mybir.ActivationFunctionType
ALU = mybir.AluOpType
AX = mybir.AxisListType

D = 2048
DC = 16
DFF = 5632
NG = DFF // 256
TO = 1024
TA = 1040
NM = 16
LSEQ = 4112
EPS = 1e-6
NEG = -30000.0
INW = 10248
C_FQ, C_FK, C_FV, C_F, C_SQ, C_SK, C_SV, C_GF, C_GS = 0, 1024, 2048, 3072, 3080, 4104, 5128, 6152, 8200


class Res:
    def __init__(self, name=""):
        self.name = name
        self.w = None
        self.r = {}
        self.dsem = None
        self.dcnt = 0


class Ctx:
    def __init__(self, nc):
        self.nc = nc
        self.E = {"pe": nc.tensor, "act": nc.scalar, "dve": nc.vector, "pool": nc.gpsimd, "sp": nc.sync}
        self.esem = {e: nc.alloc_semaphore(name="es_" + e) for e in ("pe", "act", "dve", "pool")}
        self.ecnt = {e: 0 for e in self.esem}
        self.seen = {}
        self.all_res = []
        self.semkey = {}
        self.cc_sem = nc.alloc_semaphore(name="cc_sem")
        self.cc_cnt = 0

    def res(self, name=""):
        r = Res(name)
        self.all_res.append(r)
        return r

    def _k(self, sem):
        return id(sem)

    def wait(self, eng, tok):
        if tok is None:
            return
        sem, val = tok
        if eng == "pe" and sem is self.esem["pe"]:
            return
        key = (eng, self._k(sem))
        if self.seen.get(key, 0) >= val:
            return
        self.seen[key] = val
        self.E[eng].wait_ge(sem, val)

    def mark(self, eng, ins):
        self.ecnt[eng] += 1
        ins.then_inc(self.esem[eng], 1)
        return (self.esem[eng], self.ecnt[eng])

    def pre(self, eng, reads, writes):
        for r in reads:
            self.wait(eng, r.w)
        for w in writes:
            self.wait(eng, w.w)
            for t in list(w.r.values()):
                self.wait(eng, t)

    def post(self, tok, reads, writes):
        for r in reads:
            r.r[self._k(tok[0])] = tok
        for w in writes:
            w.w = tok
            w.r = {}

    def op(self, eng, fn, reads=(), writes=(), mark=True):
        self.pre(eng, reads, writes)
        ins = fn()
        if mark:
            tok = self.mark(eng, ins)
            self.post(tok, reads, writes)
            return tok
        return None

    def dma(self, q, out, in_, reads=(), writes=(), **kw):
        self.pre(q, reads, writes)
        res = writes[0]
        if res.dsem is None:
            self.nsem = getattr(self, "nsem", 0) + 1
            res.dsem = self.nc.alloc_semaphore(name="ds%d_%s" % (self.nsem, res.name))
        res.dcnt += 16
        self.E[q].dma_start(out=out, in_=in_, **kw).then_inc(res.dsem, 16)
        tok = (res.dsem, res.dcnt)
        self.post(tok, reads, writes)
        return tok

    def collective(self, ins_ap, outs_ap, reads, writes, groups):
        self.pre("pool", reads, writes)
        self.cc_cnt += 1
        self.nc.gpsimd.collective_compute(
            "AllGather", ALU.bypass, replica_groups=groups, ins=[ins_ap], outs=[outs_ap], dma_qos="P3"
        ).then_inc(self.cc_sem, 1)
        tok = (self.cc_sem, self.cc_cnt)
        self.post(tok, reads, writes)
        return tok

    def barrier(self, skip_cc=False):
        toks = [(self.esem[e], self.ecnt[e]) for e in self.esem if self.ecnt[e] > 0]
        for r in self.all_res:
            if r.dsem is not None and r.dcnt > 0:
                toks.append((r.dsem, r.dcnt))
        if self.cc_cnt and not skip_cc:
            toks.append((self.cc_sem, self.cc_cnt))
        for eng in ("pe", "act", "dve", "pool", "sp"):
            for t in toks:
                self.wait(eng, t)


class Arena:
    def __init__(self, tens, nbytes):
        self.t = tens
        self.nbytes = nbytes

    def carve(self, off, shape, dt, parts=128):
        esz = 4 if dt == F32 else 2
        n = int(np.prod(shape)) * esz
        assert off % 4 == 0 and off + n <= self.nbytes, (off, n, self.nbytes)
        a = self.t[0:parts, off // 2:(off + n) // 2]
        if dt == F32:
            a = a.bitcast(F32)
        if len(shape) == 2:
            a = a.rearrange("p (a b) -> p a b", a=shape[0])
        elif len(shape) == 3:
            a = a.rearrange("p (a b c) -> p a b c", a=shape[0], b=shape[1])
        return a


ARENA_BYTES = 212000


def build(mode="full", ngroups=2):
    nc = bass.Bass("TRN2", target_bir_lowering=False)
    K = Ctx(nc)

    def din(name, shape, dt=F32):
        return nc.dram_tensor(name, list(shape), dt, kind="ExternalInput")

    FULL = (mode == "full")
    GROUPS = [[0, 1, 2, 3], [4, 5, 6, 7]][:ngroups]

    xT = din("xT", [D, TA])
    g1 = din("g1", [128, DC]); g2 = din("g2", [128, DC]); g3 = din("g3", [128, DC])
    gq = din("gq", [128, 8]); gk = din("gk", [128, 8]); bfo = din("bfo", [8, 1])
    if FULL:
        w1g = din("w1g", [D, DFF]); w1u = din("w1u", [D, DFF]); w1d = din("w1d", [DFF, D])
        w2g = din("w2g", [D, DFF]); w2u = din("w2u", [D, DFF]); w2d = din("w2d", [DFF, D])
    win = din("win", [D, INW]); wbf = din("wbf", [1024, D]); wbs = din("wbs", [1024, D]); wo = din("wo", [D, D])
    cmat = din("cmat", [128, 4, 128], BF16)
    cones = din("cones", [128, 128], F32)
    cdiag = din("cdiag", [128, 8, 512], BF16)
    cisel = din("cisel", [128, 8, 128], BF16)
    ckmask = din("ckmask", [2, LSEQ], BF16)
    cqsel = din("cqsel", [2, TO], BF16)
    cpresel = din("cpresel", [8, 2, 9], F32)
    outT = nc.dram_tensor("outT", [D, TO], F32, kind="ExternalOutput")
    hspill = nc.dram_tensor("hspill", [D, TO], F32)

    arena_cm = nc.sbuf_tensor("arena", [128, ARENA_BYTES // 2], BF16)
    arena_t = arena_cm.__enter__()
    AR = Arena(arena_t, ARENA_BYTES)
    ps_cms = [nc.psum_tensor("ps%d" % i, [128, 512], F32) for i in range(8)]
    PS = [cm.__enter__() for cm in ps_cms]
    PSR = [K.res("ps%d" % i) for i in range(8)]

    o = 0
    def alloc(shape, dt, parts=128):
        nonlocal o
        esz = 4 if dt == F32 else 2
        a = AR.carve(o, shape, dt, parts)
        o += (int(np.prod(shape)) * esz + 3) // 4 * 4
        return a
    ones_f = alloc([128], F32)
    cm_sb = alloc([4, 128], BF16)
    ones_b, ident_b, tri_b, tric_b = cm_sb[:, 0, :], cm_sb[:, 1, :], cm_sb[:, 2, :], cm_sb[:, 3, :]
    g1_sb = alloc([DC], F32); g2_sb = alloc([DC], F32); g3_sb = alloc([DC], F32)
    gq_sb = alloc([8], F32); gk_sb = alloc([8], F32); bfo_sb = alloc([1], F32); nb_sb = alloc([1], F32); gqs_sb = alloc([8], F32)
    diag_sb = alloc([8, 512], BF16)
    isel_sb = alloc([8, 128], BF16)
    kmask_sb = alloc([LSEQ], BF16)
    qsel_sb = alloc([TO], BF16)
    presel_sb = alloc([2, 9], F32)
    metaKf = alloc([8, NM], BF16); metaKs = alloc([8, NM], BF16)
    metaVf = alloc([1024], BF16); metaVs = alloc([1024], BF16)
    sp_own = alloc([TA], F32)
    xn = alloc([DC, TA], BF16)
    G_END = o
    H_OFF = G_END
    h = AR.carve(H_OFF, [DC, TA], F32)
    S_OFF = H_OFF + DC * TA * 4
    CONST = K.res("const")
    XNS = {"tiles": [], "res": []}

    def xnr(t0, tn):
        return [r for (a, n_), r in zip(XNS["tiles"], XNS["res"]) if a < t0 + tn and t0 < a + n_]
    HR = [[K.res("h%d_%d" % (c, t)) for t in range(3)] for c in range(DC)]

    C_A = K.res("constA")
    K.dma("sp", ones_f, cones.ap(), writes=[C_A])
    K.dma("sp", g1_sb, g1.ap(), writes=[C_A])
    TILES3 = [(0, 352), (352, 352), (704, 336)]
    TILES2 = [(0, 512), (512, 512)]
    xT_v = xT.ap().rearrange("(c p) t -> p c t", p=128)
    HT = [K.res("ht%d" % i) for i in range(3)]
    for ti, (t0, tn) in enumerate(TILES3):
        K.dma("sp", h[:, :, t0:t0 + tn], xT_v[:, :, t0:t0 + tn], writes=[HT[ti]])
        for c in range(DC):
            HR[c][ti].w = HT[ti].w
    C_B = K.res("constB")
    K.dma("sp", cm_sb, cmat.ap(), writes=[C_B])
    for sb, dr in ((g2_sb, g2), (g3_sb, g3), (gq_sb, gq), (gk_sb, gk)):
        K.dma("sp", sb, dr.ap(), writes=[C_B])
    K.dma("sp", bfo_sb[0:8, :], bfo.ap(), writes=[C_B])
    K.dma("sp", diag_sb, cdiag.ap(), writes=[CONST])
    K.dma("sp", isel_sb, cisel.ap(), writes=[CONST])
    K.op("dve", lambda: nc.vector.memset(kmask_sb[:, :], 0.0), writes=[CONST])
    K.op("dve", lambda: nc.vector.memset(qsel_sb[:, :], 0.0), writes=[CONST])
    K.dma("sp", kmask_sb[0:2, :], ckmask.ap(), writes=[CONST])
    K.dma("sp", qsel_sb[0:2, :], cqsel.ap(), writes=[CONST])
    K.dma("sp", presel_sb[0:8, :, :], cpresel.ap(), writes=[CONST])
    HALL = K.res("hall")

    def rmsnorm(gain_sb, tiles, soff):
        sq = [AR.carve(soff + i * 2048, [512], F32) for i in range(2)]
        lnv = AR.carve(soff + 4096, [512], F32)
        rstd = AR.carve(soff + 6144, [512], F32)
        SQ = [K.res("sq0"), K.res("sq1")]
        LNV = K.res("lnv"); RSTD = K.res("rstd")
        XNS["tiles"] = list(tiles)
        XNS["res"] = [K.res("xn%d" % i) for i in range(len(tiles))]
        for ti, (t0, tn) in enumerate(tiles):
            pb = 6 + (ti % 2)
            for c in range(DC):
                K.op("act", lambda: nc.scalar.activation(out=sq[c % 2][:, 0:tn], in_=h[:, c, t0:t0 + tn], func=AF.Square),
                     reads=[HR[c][ti]], writes=[SQ[c % 2]])
                K.op("pe", lambda: nc.tensor.matmul(PS[pb][:, 0:tn], ones_f, sq[c % 2][:, 0:tn], start=(c == 0), stop=(c == DC - 1)),
                     reads=[SQ[c % 2], C_A], writes=[PSR[pb]])
            K.op("act", lambda: nc.scalar.activation(out=lnv[:, 0:tn], in_=PS[pb][:, 0:tn], func=AF.Ln, scale=1.0 / D, bias=EPS),
                 reads=[PSR[pb]], writes=[LNV])
            K.op("act", lambda: nc.scalar.activation(out=rstd[:, 0:tn], in_=lnv[:, 0:tn], func=AF.Exp, scale=-0.5),
                 reads=[LNV], writes=[RSTD])
            for c in range(DC):
                K.op("dve", lambda: nc.vector.scalar_tensor_tensor(out=xn[:, c, t0:t0 + tn], in0=h[:, c, t0:t0 + tn], scalar=gain_sb[:, c:c + 1],
                                                                   in1=rstd[:, 0:tn], op0=ALU.mult, op1=ALU.mult),
                     reads=[HR[c][ti], RSTD, C_A], writes=[XNS["res"][ti]])

    def ffn(wg, wu, wd, tiles, soff):
        wA = [AR.carve(soff + i * 8192, [DC, 256], BF16) for i in range(2)]
        wB = [AR.carve(soff + 16384 + i * 8192, [DC, 256], BF16) for i in range(2)]
        wD = [AR.carve(soff + 32768 + i * 8192, [2, D], BF16) for i in range(2)]
        act = [AR.carve(soff + 49152 + i * 4160, [2, TA], BF16) for i in range(2)]
        sg = [AR.carve(soff + 57472 + i * 2048, [512], F32) for i in range(2)]
        WA = [K.res("wA0"), K.res("wA1")]; WB = [K.res("wB0"), K.res("wB1")]; WD = [K.res("wD0"), K.res("wD1")]
        ACT = [K.res("act0"), K.res("act1")]; SG = [K.res("sg0"), K.res("sg1")]
        wg_v = wg.ap().rearrange("(c p) f -> p c f", p=128)
        wu_v = wu.ap().rearrange("(c p) f -> p c f", p=128)

        def load_ab(g):
            s = g % 2
            K.dma("pool", wA[s], wg_v[:, :, g * 256:(g + 1) * 256], writes=[WA[s]])
            K.dma("pool", wB[s], wu_v[:, :, g * 256:(g + 1) * 256], writes=[WB[s]])

        def load_d(g):
            s = g % 2
            K.dma("pool", wD[s], wd.ap()[g * 256:(g + 1) * 256, :].rearrange("(fc p) d -> p fc d", p=128), writes=[WD[s]])

        cnt = [0]

        def part_a(g):
            s = g % 2
            for fc in range(2):
                for ti, (t0, tn) in enumerate(tiles):
                    i = cnt[0] % 2
                    cnt[0] += 1
                    pg, pu = i, 2 + i
                    for c in range(DC):
                        K.op("pe", lambda: nc.tensor.matmul(PS[pg][:, 0:tn], wA[s][:, c, fc * 128:(fc + 1) * 128], xn[:, c, t0:t0 + tn], start=(c == 0), stop=(c == DC - 1)),
                             reads=[WA[s]] + xnr(t0, tn), writes=[PSR[pg]], mark=(c == DC - 1))
                    yield
                    for c in range(DC):
                        K.op("pe", lambda: nc.tensor.matmul(PS[pu][:, 0:tn], wB[s][:, c, fc * 128:(fc + 1) * 128], xn[:, c, t0:t0 + tn], start=(c == 0), stop=(c == DC - 1)),
                             reads=[WB[s]] + xnr(t0, tn), writes=[PSR[pu]], mark=(c == DC - 1))
                    K.op("act", lambda: nc.scalar.activation(out=sg[i][:, 0:tn], in_=PS[pg][:, 0:tn], func=AF.Silu), reads=[PSR[pg]], writes=[SG[i]])
                    K.op("dve", lambda: nc.vector.tensor_tensor(out=act[s][:, fc, t0:t0 + tn], in0=sg[i][:, 0:tn], in1=PS[pu][:, 0:tn], op=ALU.mult),
                         reads=[SG[i], PSR[pu]], writes=[ACT[s]])
                    yield

        dcnt = [0]

        def part_b(g):
            s = g % 2
            for dc in range(DC):
                for ti, (t0, tn) in enumerate(tiles):
                    pd = 4 + dcnt[0] % 4
                    dcnt[0] += 1
                    for fc in range(2):
                        K.op("pe", lambda: nc.tensor.matmul(PS[pd][:, 0:tn], wD[s][:, fc, dc * 128:(dc + 1) * 128], act[s][:, fc, t0:t0 + tn], start=(fc == 0), stop=(fc == 1)),
                             reads=[WD[s], ACT[s]], writes=[PSR[pd]], mark=(fc == 1))
                    K.op("dve", lambda: nc.vector.scalar_tensor_tensor(out=h[:, dc, t0:t0 + tn], in0=PS[pd][:, 0:tn], scalar=0.5, in1=h[:, dc, t0:t0 + tn], op0=ALU.mult, op1=ALU.add),
                         reads=[PSR[pd]], writes=[HR[dc][ti]])
                    yield

        def run(gen):
            for _ in gen:
                pass

        load_ab(0); load_d(0); load_ab(1); load_d(1)
        run(part_a(0))
        for g in range(NG):
            gb = part_b(g)
            if g + 1 < NG:
                for _ in part_a(g + 1):
                    for _k in range(4):
                        next(gb, None)
            run(gb)
            if g + 2 < NG:
                load_ab(g + 2)
                load_d(g + 2)

    QSCALE = 128.0 ** -0.5

    if FULL:
        rmsnorm(g1_sb, TILES3, S_OFF + 61568)
        ffn(w1g, w1u, w1d, TILES3, S_OFF)
        K.barrier()

    K.op("dve", lambda: nc.vector.tensor_scalar(out=nb_sb[0:8, :], in0=bfo_sb[0:8, :], scalar1=-1.0, scalar2=None, op0=ALU.mult),
         reads=[C_B], writes=[CONST])
    K.op("dve", lambda: nc.vector.tensor_scalar(out=gqs_sb, in0=gq_sb, scalar1=QSCALE, scalar2=None, op0=ALU.mult),
         reads=[C_B], writes=[CONST])
    rmsnorm(g2_sb, TILES3, S_OFF + 61504)
    HSP = K.res("hspill")
    hsp_v = hspill.ap().rearrange("(c p) t -> p c t", p=128)
    for c4 in range(4):
        K.dma("sp", hsp_v[:, 4 * c4:4 * c4 + 4, :], h[:, 4 * c4:4 * c4 + 4, 0:TO],
              reads=[HR[c][t] for c in range(4 * c4, 4 * c4 + 4) for t in range(3)], writes=[HSP])
    K.barrier()

    qf = AR.carve(H_OFF, [8, TO], BF16); qs = AR.carve(H_OFF + 16384, [8, TO], BF16)
    of = AR.carve(H_OFF + 32768, [8, TO], BF16); osb = AR.carve(H_OFF + 49152, [8, TO], BF16)
    QF = K.res("qf"); QS = K.res("qs"); OF = K.res("of"); OS = K.res("os")
    METAR = K.res("meta")
    SPO = K.res("spown")

    agin_k = {t: [nc.dram_tensor("agin_k%s%d" % (t, p), [1024, 512], BF16) for p in range(2)] for t in "fs"}
    agin_v = {t: [nc.dram_tensor("agin_v%s%d" % (t, p), [512, 1024], BF16) for p in range(2)] for t in "fs"}
    agout_k = {t: [nc.dram_tensor("agout_k%s%d" % (t, p), [4 * 1024, 512], BF16) for p in range(2)] for t in "fs"}
    agout_v = {t: [nc.dram_tensor("agout_v%s%d" % (t, p), [4 * 512, 1024], BF16) for p in range(2)] for t in "fs"}
    agin_f = nc.dram_tensor("agin_f", [8, TO], F32)
    agout_f = nc.dram_tensor("agout_f", [32, TO], F32)
    AGIN_K = {t: [K.res("agink%s%d" % (t, p)) for p in range(2)] for t in "fs"}
    AGIN_V = {t: [K.res("aginv%s%d" % (t, p)) for p in range(2)] for t in "fs"}
    AGOUT_K = {t: [K.res("agoutk%s%d" % (t, p)) for p in range(2)] for t in "fs"}
    AGOUT_V = {t: [K.res("agoutv%s%d" % (t, p)) for p in range(2)] for t in "fs"}
    AGIN_F = K.res("aginf"); AGOUT_F = K.res("agoutf")

    win_v = win.ap().rearrange("(c p) f -> p c f", p=128)

    def phase2():
        so = S_OFF
        wA = [AR.carve(so + i * 8192, [DC, 256], BF16) for i in range(2)]
        stg = [AR.carve(so + 16384 + i * 16384, [DC, 256], F32) for i in range(2)]
        kst = [AR.carve(so + 49152 + i * 2080, [TA], BF16) for i in range(2)]
        vst = [AR.carve(so + 53312 + i * 4096, [8, 256], BF16) for i in range(2)]
        sqb = [AR.carve(so + 61504 + i * 2048, [512], F32) for i in range(2)]
        lnv = AR.carve(so + 65600, [512], F32)
        rstd = AR.carve(so + 67648, [512], F32)
        fE = AR.carve(so + 69696, [TA], F32)
        WA = [K.res("p2wA%d" % i) for i in range(2)]
        STG = [K.res("p2stg%d" % i) for i in range(2)]
        KST = [K.res("kst0"), K.res("kst1")]; VST = [K.res("vst0"), K.res("vst1")]
        SQB = [K.res("sqb0"), K.res("sqb1")]; LNV = K.res("p2lnv"); RSTD = K.res("p2rstd"); FE = K.res("fE")

        groups = [("f", C_F, 0, 8)]
        for i in range(4):
            groups.append(("kf", C_FK + 256 * i, i, 256))
        for i in range(4):
            groups.append(("vf", C_FV + 256 * i, i, 256))
        for i in range(4):
            groups.append(("ks", C_SK + 256 * i, i, 256))
        for i in range(4):
            groups.append(("vs", C_SV + 256 * i, i, 256))
        n_kv = len(groups)
        for i in range(4):
            groups.append(("qf", C_FQ + 256 * i, i, 256))
        for i in range(4):
            groups.append(("qs", C_SQ + 256 * i, i, 256))

        def load(gi):
            kind, c0, i, ncol = groups[gi]
            K.dma("sp", stg[gi % 2][:, :, 0:ncol], win_v[:, :, c0:c0 + ncol], writes=[STG[gi % 2]])

        def convert(gi):
            kind, c0, i, ncol = groups[gi]
            K.op("dve", lambda: nc.vector.tensor_copy(out=wA[gi % 2][:, :, 0:ncol], in_=stg[gi % 2][:, :, 0:ncol]), reads=[STG[gi % 2]], writes=[WA[gi % 2]])

        rot = [0]
        hcnt = [0]
        vcnt = [0]

        def normed(pk, pidx, tn, gain_ap, dest, dres):
            i2 = rot[0] % 2
            K.op("act", lambda: nc.scalar.activation(out=sqb[i2][:, 0:tn], in_=PS[pidx][:, 0:tn], func=AF.Square), reads=[PSR[pidx]], writes=[SQB[i2]])
            K.op("pe", lambda: nc.tensor.matmul(PS[6 + i2][:, 0:tn], ones_f, sqb[i2][:, 0:tn], start=True, stop=True), reads=[SQB[i2], CONST], writes=[PSR[6 + i2]])
            K.op("act", lambda: nc.scalar.activation(out=lnv[:, 0:tn], in_=PS[6 + i2][:, 0:tn], func=AF.Ln, scale=1.0 / 128, bias=EPS), reads=[PSR[6 + i2]], writes=[LNV])
            K.op("act", lambda: nc.scalar.activation(out=rstd[:, 0:tn], in_=lnv[:, 0:tn], func=AF.Exp, scale=-0.5), reads=[LNV], writes=[RSTD])
            K.op("dve", lambda: nc.vector.scalar_tensor_tensor(out=dest, in0=PS[pidx][:, 0:tn], scalar=gain_ap, in1=rstd[:, 0:tn], op0=ALU.mult, op1=ALU.mult),
                 reads=[PSR[pidx], RSTD, CONST], writes=[dres])

        def compute(gi):
            kind, c0, i, ncol = groups[gi]
            s = gi % 2
            if kind == "f":
                for ti, (t0, tn) in enumerate(TILES3):
                    for c in range(DC):
                        K.op("pe", lambda: nc.tensor.matmul(PS[4][0:8, 0:tn], wA[s][:, c, 0:8], xn[:, c, t0:t0 + tn], start=(c == 0), stop=(c == DC - 1)),
                             reads=[WA[s]] + xnr(t0, tn), writes=[PSR[4]], mark=(c == DC - 1))
                    K.op("act", lambda: nc.scalar.activation(out=fE[0:8, t0:t0 + tn], in_=PS[4][0:8, 0:tn], func=AF.Exp, scale=-1.0, bias=nb_sb[0:8, :]),
                         reads=[PSR[4], CONST], writes=[FE])
                    K.op("act", lambda: nc.scalar.activation(out=sp_own[0:8, t0:t0 + tn], in_=fE[0:8, t0:t0 + tn], func=AF.Ln, bias=1.0),
                         reads=[FE], writes=[SPO])
                K.dma("pool", agin_f.ap(), sp_own[0:8, 0:TO], reads=[SPO], writes=[AGIN_F])
            elif kind in ("kf", "ks", "qf", "qs"):
                typ = kind[1]
                isq = kind[0] == "q"
                tiles = TILES2 if isq else TILES3
                for hh in range(2):
                    head = 2 * i + hh
                    hb = hcnt[0] % 2
                    hcnt[0] += 1
                    for ti, (t0, tn) in enumerate(tiles):
                        pidx = rot[0] % 4
                        rot[0] += 1
                        for c in range(DC):
                            K.op("pe", lambda: nc.tensor.matmul(PS[pidx][:, 0:tn], wA[s][:, c, hh * 128:(hh + 1) * 128], xn[:, c, t0:t0 + tn], start=(c == 0), stop=(c == DC - 1)),
                                 reads=[WA[s]] + xnr(t0, tn), writes=[PSR[pidx]], mark=(c == DC - 1))
                        if isq:
                            dest = (qf if typ == "f" else qs)[:, head, t0:t0 + tn]
                            dres = QF if typ == "f" else QS
                        else:
                            dest = kst[hb][:, t0:t0 + tn]
                            dres = KST[hb]
                        if typ == "f":
                            gain_ap = (gqs_sb if isq else gk_sb)[:, head:head + 1]
                            normed(PS[pidx], pidx, tn, gain_ap, dest, dres)
                        elif isq:
                            K.op("dve", lambda: nc.vector.tensor_scalar(out=dest, in0=PS[pidx][:, 0:tn], scalar1=QSCALE, scalar2=None, op0=ALU.mult),
                                 reads=[PSR[pidx]], writes=[dres])
                        else:
                            K.op("dve", lambda: nc.vector.tensor_copy(out=dest, in_=PS[pidx][:, 0:tn]), reads=[PSR[pidx]], writes=[dres])
                    if not isq:
                        mk = metaKf if typ == "f" else metaKs
                        K.op("dve", lambda: nc.vector.tensor_copy(out=mk[:, head, :], in_=kst[hb][:, TO:TA]), reads=[KST[hb]], writes=[METAR])
                        for pos in range(2):
                            K.dma("pool", agin_k[typ][pos].ap()[head * 128:(head + 1) * 128, :], kst[hb][:, pos * 512:(pos + 1) * 512],
                                  reads=[KST[hb]], writes=[AGIN_K[typ][pos]])
            else:
                typ = kind[1]
                vb = vcnt[0] % 2
                vcnt[0] += 1
                mv = metaVf if typ == "f" else metaVs
                for tb in range(9):
                    M = 128 if tb < 8 else NM
                    pidx = rot[0] % 4
                    rot[0] += 1
                    for c in range(DC):
                        K.op("pe", lambda: nc.tensor.matmul(PS[pidx][0:M, 0:256], xn[:, c, tb * 128:tb * 128 + M], wA[s][:, c, 0:256], start=(c == 0), stop=(c == DC - 1)),
                             reads=[WA[s]] + xnr(tb * 128, M), writes=[PSR[pidx]], mark=(c == DC - 1))
                    if tb < 8:
                        K.op("act", lambda: nc.scalar.copy(out=vst[vb][:, tb, :], in_=PS[pidx][:, 0:256]), reads=[PSR[pidx]], writes=[VST[vb]])
                    else:
                        K.op("act", lambda: nc.scalar.copy(out=mv[0:NM, i * 256:(i + 1) * 256], in_=PS[pidx][0:NM, 0:256]), reads=[PSR[pidx]], writes=[METAR])
                for pos in range(2):
                    K.dma("pool", agin_v[typ][pos].ap()[:, i * 256:(i + 1) * 256].rearrange("(tb p) f -> p tb f", p=128), vst[vb][:, 4 * pos:4 * pos + 4, :],
                          reads=[VST[vb]], writes=[AGIN_V[typ][pos]])

        def collectives(gi):
            kind, c0, i, ncol = groups[gi]
            if kind == "f":
                K.collective(agin_f.ap().opt(), agout_f.ap().opt(), [AGIN_F], [AGOUT_F], GROUPS)
            elif i == 3 and kind[0] in "kv":
                typ = kind[1]
                for pos in range(2):
                    if kind[0] == "k":
                        K.collective(agin_k[typ][pos].ap().opt(), agout_k[typ][pos].ap().opt(), [AGIN_K[typ][pos]], [AGOUT_K[typ][pos]], GROUPS)
                    else:
                        K.collective(agin_v[typ][pos].ap().opt(), agout_v[typ][pos].ap().opt(), [AGIN_V[typ][pos]], [AGOUT_V[typ][pos]], GROUPS)

        load(0); load(1); convert(0)
        for gi in range(len(groups)):
            if gi + 1 < len(groups):
                convert(gi + 1)
            compute(gi)
            if gi + 2 < len(groups):
                load(gi + 2)
            collectives(gi)

    phase2()
    K.barrier(skip_cc=True)

    cs_k = nc.dram_tensor("cs_k", [3, 8, LSEQ], BF16)
    cs_q = nc.dram_tensor("cs_q", [3, 8, TO], BF16)
    CSK = K.res("csk"); CSQ = K.res("csq")

    def chunk_src(ch):
        return (ch, 0) if ch < 4 else (7 - ch, 1)

    def phase25():
        so = S_OFF
        spg = AR.carve(so, [LSEQ], F32)
        Cg = AR.carve(so + 16448, [LSEQ], F32)
        kp = [AR.carve(so + 32896 + i * 8224, [LSEQ], BF16) for i in range(3)]
        Cown = AR.carve(so + 57568, [TO], F32)
        qp = [AR.carve(so + 61664 + i * 2048, [TO], BF16) for i in range(3)]
        sbv = AR.carve(so + 67808, [2], F32)
        SPG = K.res("spg"); CG = K.res("cg"); KP = K.res("kp"); COWN = K.res("cown"); QP = K.res("qp"); SBV = K.res("sbv")
        K.dma("sp", spg[0:8, 0:NM], sp_own[0:8, TO:TA], reads=[SPO], writes=[SPG])
        for ch in range(8):
            rank, pos = chunk_src(ch)
            K.dma("sp", spg[0:8, NM + ch * 512:NM + (ch + 1) * 512], agout_f.ap()[rank * 8:rank * 8 + 8, pos * 512:(pos + 1) * 512],
                  reads=[AGOUT_F], writes=[SPG])
        K.op("dve", lambda: nc.vector.tensor_tensor_scan(out=Cg[0:8, :], data0=spg[0:8, :], data1=spg[0:8, :], initial=0.0, op0=ALU.add, op1=ALU.bypass),
             reads=[SPG], writes=[CG])
        K.op("dve", lambda: nc.vector.memset(sbv[0:8, :], 0.0), writes=[SBV])
        for qi in range(2):
            for k in range(9):
                pos = NM - 1 + 512 * k
                K.op("dve", lambda: nc.vector.scalar_tensor_tensor(out=sbv[0:8, qi:qi + 1], in0=Cg[0:8, pos:pos + 1], scalar=presel_sb[0:8, qi, k:k + 1],
                                                                   in1=sbv[0:8, qi:qi + 1], op0=ALU.mult, op1=ALU.add),
                     reads=[CG, CONST, SBV], writes=[SBV])
        for qi in range(2):
            K.op("dve", lambda: nc.vector.tensor_tensor_scan(out=Cown[0:8, qi * 512:(qi + 1) * 512], data0=sp_own[0:8, qi * 512:(qi + 1) * 512],
                                                            data1=sp_own[0:8, qi * 512:(qi + 1) * 512], initial=sbv[0:8, qi:qi + 1], op0=ALU.add, op1=ALU.bypass),
                 reads=[SPO, SBV], writes=[COWN])
        K.op("dve", lambda: nc.vector.tensor_copy(out=kp[0][0:8, :], in_=Cg[0:8, :]), reads=[CG], writes=[KP])
        K.op("dve", lambda: nc.vector.tensor_tensor(out=spg[0:8, :], in0=Cg[0:8, :], in1=kp[0][0:8, :], op=ALU.subtract), reads=[CG, KP], writes=[SPG])
        K.op("dve", lambda: nc.vector.tensor_copy(out=kp[1][0:8, :], in_=spg[0:8, :]), reads=[SPG], writes=[KP])
        K.op("dve", lambda: nc.vector.tensor_tensor(out=spg[0:8, :], in0=spg[0:8, :], in1=kp[1][0:8, :], op=ALU.subtract), reads=[SPG, KP], writes=[SPG])
        K.op("dve", lambda: nc.vector.tensor_copy(out=kp[2][0:8, :], in_=spg[0:8, :]), reads=[SPG], writes=[KP])
        K.op("dve", lambda: nc.vector.tensor_scalar(out=Cown[0:8, :], in0=Cown[0:8, :], scalar1=-1.0, scalar2=None, op0=ALU.mult), reads=[COWN], writes=[COWN])
        K.op("dve", lambda: nc.vector.tensor_copy(out=qp[0][0:8, :], in_=Cown[0:8, :]), reads=[COWN], writes=[QP])
        K.op("dve", lambda: nc.vector.tensor_tensor(out=Cown[0:8, :], in0=Cown[0:8, :], in1=qp[0][0:8, :], op=ALU.subtract), reads=[COWN, QP], writes=[COWN])
        K.op("dve", lambda: nc.vector.tensor_copy(out=qp[1][0:8, :], in_=Cown[0:8, :]), reads=[COWN], writes=[QP])
        K.op("dve", lambda: nc.vector.tensor_tensor(out=Cown[0:8, :], in0=Cown[0:8, :], in1=qp[1][0:8, :], op=ALU.subtract), reads=[COWN, QP], writes=[COWN])
        K.op("dve", lambda: nc.vector.tensor_copy(out=qp[2][0:8, :], in_=Cown[0:8, :]), reads=[COWN], writes=[QP])
        for i in range(3):
            K.dma("sp", cs_k.ap()[i], kp[i][0:8, :], reads=[KP], writes=[CSK])
            K.dma("sp", cs_q.ap()[i], qp[i][0:8, :], reads=[QP], writes=[CSQ])

    phase25()
    K.barrier(skip_cc=True)

    def phase3():
        so = S_OFF
        kt = [AR.carve(so + i * 8192, [8, 512], BF16) for i in range(2)]
        vv = [AR.carve(so + 16384 + i * 8192, [32, 128], BF16) for i in range(2)]
        kaug = [AR.carve(so + 32768 + i * 8224, [LSEQ], BF16) for i in range(2)]
        qaug = [AR.carve(so + 49216 + i * 2048, [TO], BF16) for i in range(2)]
        Pb = [AR.carve(so + 53312 + i * 1024, [512], BF16) for i in range(2)]
        Eb = [AR.carve(so + 55360 + i * 2048, [512], F32) for i in range(2)]
        Lb = [AR.carve(so + 59456 + i * 1024, [512], BF16) for i in range(2)]
        Tb = [AR.carve(so + 61504 + i * 2048, [512], F32) for i in range(2)]
        Xb = [AR.carve(so + 65600 + i * 2048, [512], F32) for i in range(2)]
        Wb = [AR.carve(so + 69696 + i * 1024, [512], BF16) for i in range(2)]
        rden = AR.carve(so + 71744, [512], F32)
        KTC = [[K.res("kt%d_%d" % (t, c)) for c in range(8)] for t in range(2)]
        VVC = [[K.res("vv%d_%d" % (t, c)) for c in range(8)] for t in range(2)]
        KAUG = [K.res("kaug0"), K.res("kaug1")]; QAUG = [K.res("qaug0"), K.res("qaug1")]
        PB = [K.res("pb0"), K.res("pb1")]; EB = [K.res("eb0"), K.res("eb1")]; LB = [K.res("lb0"), K.res("lb1")]
        TB = [K.res("tb0"), K.res("tb1")]; XB = [K.res("xb0"), K.res("xb1")]; WB_ = [K.res("wb0"), K.res("wb1")]
        RDEN = K.res("rden")
        for i in range(2):
            K.op("dve", lambda: nc.vector.memset(kaug[i][:, :], 0.0), writes=[KAUG[i]])
            K.op("dve", lambda: nc.vector.memset(qaug[i][:, :], 0.0), writes=[QAUG[i]])
            K.op("dve", lambda: nc.vector.memset(kaug[i][0:8, :], 1.0), writes=[KAUG[i]])
            K.op("dve", lambda: nc.vector.memset(qaug[i][0:8, :], 1.0), writes=[QAUG[i]])
            K.dma("sp", kaug[i][6:8, :], ckmask.ap(), writes=[KAUG[i]])
            K.dma("sp", qaug[i][6:8, :], cqsel.ap(), writes=[QAUG[i]])

        def load_chunk(t, typ, h, ch):
            rank, pos = chunk_src(ch)
            K.dma("sp", kt[t][:, ch, :], agout_k[typ][pos].ap()[rank * 1024 + h * 128:rank * 1024 + (h + 1) * 128, :],
                  reads=[AGOUT_K[typ][pos]], writes=[KTC[t][ch]])
            K.dma("sp", vv[t][:, 4 * ch:4 * ch + 4, :],
                  agout_v[typ][pos].ap()[rank * 512:(rank + 1) * 512, h * 128:(h + 1) * 128].rearrange("(kb p) d -> p kb d", p=128),
                  reads=[AGOUT_V[typ][pos]], writes=[VVC[t][ch]])

        def load_pair(h):
            ab = h % 2
            K.dma("sp", kaug[ab][0:3, :], cs_k.ap()[:, h, :], reads=[CSK], writes=[KAUG[ab]])
            K.dma("sp", qaug[ab][3:6, :], cs_q.ap()[:, h, :], reads=[CSQ], writes=[QAUG[ab]])
            for c in range(8):
                load_chunk(0, "f", h, c)
                load_chunk(1, "s", h, [3, 2, 1, 0, 7, 6, 5, 4][c] if h == 0 else 7 - c)

        def blocks_for(qt):
            chunks = range(4) if qt == 0 else range(8)
            bl = [("meta", 0, 0)]
            for ch in chunks:
                for kb in range(4):
                    bl.append(("blk", ch, kb))
            return bl

        def unit_of(qt, ch):
            if qt == 0:
                return ch
            return ch if ch >= 4 else None

        def blk_aps(t, h, b):
            kind, ch, kb = b
            if kind == "meta":
                mk = metaKf if t == 0 else metaKs
                mv = metaVf if t == 0 else metaVs
                return NM, mk[:, h, :], mv[0:NM, h * 128:(h + 1) * 128], 0, [METAR], [METAR]
            return (128, kt[t][:, ch, kb * 128:(kb + 1) * 128], vv[t][:, 4 * ch + kb, :], NM + ch * 512 + kb * 128,
                    [KTC[t][ch]], [VVC[t][ch]])

        def pair(h):
            ab = h % 2
            for qt in range(2):
                q0 = qt * 512
                blf = blocks_for(qt)
                bls = list(reversed(blf[1:])) + [blf[0]]
                n = len(blf)

                def s_block(i):
                    kind, ch, kb = blf[i]
                    nk, kT, vB, col, kres, vres = blk_aps(0, h, blf[i])
                    si = i % 2
                    u = unit_of(qt, ch) if kind == "blk" else None
                    K.op("pe", lambda: nc.tensor.matmul(PS[si][0:nk, :], kT, qf[:, h, q0:q0 + 512], start=True, stop=False),
                         reads=kres + [QF], writes=[PSR[si]], mark=False)
                    K.op("pe", lambda: nc.tensor.matmul(PS[si][0:nk, :], kaug[ab][:, col:col + nk], qaug[ab][:, q0:q0 + 512], start=False, stop=(u is None)),
                         reads=[KAUG[ab], QAUG[ab], QF] + kres, writes=[PSR[si]], mark=(u is None))
                    if u is not None:
                        K.op("pe", lambda: nc.tensor.matmul(PS[si][0:nk, :], isel_sb[:, u, :], diag_sb[:, kb, :], start=False, stop=True),
                             reads=[CONST, KAUG[ab], QAUG[ab], QF] + kres, writes=[PSR[si]])
                    K.op("act", lambda: nc.scalar.activation(out=Pb[si][0:nk, :], in_=PS[si][0:nk, :], func=AF.Exp), reads=[PSR[si]], writes=[PB[si]])

                def pv_block(i):
                    nk, kT, vB, col, kres, vres = blk_aps(0, h, blf[i])
                    si = i % 2
                    K.op("pe", lambda: nc.tensor.matmul(PS[2][:, :], vB, Pb[si][0:nk, :], start=(i == 0), stop=(i == n - 1)),
                         reads=vres + [PB[si]], writes=[PSR[2]], mark=False)
                    K.op("pe", lambda: nc.tensor.matmul(PS[3][:, :], ones_b[0:nk, :], Pb[si][0:nk, :], start=(i == 0), stop=(i == n - 1)),
                         reads=[CONST, PB[si]] + vres, writes=[PSR[3], PSR[2]])

                def z_block(i):
                    kind, ch, kb = bls[i]
                    nk, kT, vB, col, kres, vres = blk_aps(1, h, bls[i])
                    zi = 4 + i % 2
                    j = i % 2
                    u = unit_of(qt, ch) if kind == "blk" else None
                    K.op("pe", lambda: nc.tensor.matmul(PS[zi][0:nk, :], kT, qs[:, h, q0:q0 + 512], start=True, stop=(u is None)),
                         reads=kres + [QS], writes=[PSR[zi]], mark=(u is None))
                    if u is not None:
                        K.op("pe", lambda: nc.tensor.matmul(PS[zi][0:nk, :], kmask_sb[:, col:col + nk], qsel_sb[:, q0:q0 + 512], start=False, stop=False),
                             reads=[CONST], writes=[PSR[zi]], mark=False)
                        K.op("pe", lambda: nc.tensor.matmul(PS[zi][0:nk, :], isel_sb[:, u, :], diag_sb[:, 4 + kb, :], start=False, stop=True),
                             reads=[CONST, QS] + kres, writes=[PSR[zi]])
                    K.op("act", lambda: nc.scalar.activation(out=Eb[j][0:nk, :], in_=PS[zi][0:nk, :], func=AF.Exp), reads=[PSR[zi]], writes=[EB[j]])
                    K.op("act", lambda: nc.scalar.activation(out=Lb[j][0:nk, :], in_=Eb[j][0:nk, :], func=AF.Ln, bias=1.0), reads=[EB[j]], writes=[LB[j]])

                def tril(i):
                    nk = NM if bls[i][0] == "meta" else 128
                    j = i % 2
                    K.op("pe", lambda: nc.tensor.matmul(PS[6][0:nk, :], tri_b[0:nk, 0:nk], Lb[j][0:nk, :], start=(i == 0), stop=(i == n - 1)),
                         reads=[LB[j], CONST], writes=[PSR[6]])
                    K.op("act", lambda: nc.scalar.activation(out=Xb[j][0:nk, :], in_=PS[6][0:nk, :], func=AF.Exp, scale=-1.0), reads=[PSR[6]], writes=[XB[j]])

                def tric_w(i):
                    nk = NM if bls[i][0] == "meta" else 128
                    j = i % 2
                    if i < n - 1:
                        K.op("pe", lambda: nc.tensor.matmul(PS[6][:, :], tric_b[0:nk, :], Lb[j][0:nk, :], start=False, stop=False),
                             reads=[LB[j], CONST], writes=[PSR[6]])
                    K.op("dve", lambda: nc.vector.tensor_tensor(out=Wb[j][0:nk, :], in0=Eb[j][0:nk, :], in1=Xb[j][0:nk, :], op=ALU.mult),
                         reads=[EB[j], XB[j]], writes=[WB_[j]])

                def pv_s(i):
                    nk, kT, vB, col, kres, vres = blk_aps(1, h, bls[i])
                    j = i % 2
                    K.op("pe", lambda: nc.tensor.matmul(PS[7][:, :], vB, Wb[j][0:nk, :], start=(i == 0), stop=(i == n - 1)),
                         reads=vres + [WB_[j]], writes=[PSR[7]])

                s_block(0)
                z_block(0)
                for i in range(n):
                    tril(i)
                    if i + 1 < n:
                        s_block(i + 1)
                        z_block(i + 1)
                    pv_block(i)
                    if i > 0:
                        pv_s(i - 1)
                    tric_w(i)
                pv_s(n - 1)
                K.op("dve", lambda: nc.vector.reciprocal(out=rden, in_=PS[3][:, :]), reads=[PSR[3]], writes=[RDEN])
                K.op("dve", lambda: nc.vector.tensor_tensor(out=of[:, h, q0:q0 + 512], in0=PS[2][:, :], in1=rden, op=ALU.mult),
                     reads=[PSR[2], PSR[3], RDEN], writes=[OF])
                K.op("dve", lambda: nc.vector.tensor_copy(out=osb[:, h, q0:q0 + 512], in_=PS[7][:, :]),
                     reads=[PSR[7]], writes=[OS])

        load_pair(0)
        for h in range(8):
            pair(h)
            if h + 1 < 8:
                load_pair(h + 1)

    phase3()
    K.barrier()

    merged = AR.carve(S_OFF, [DC, TO], BF16)
    MERGED = K.res("merged")
    wbf_v = wbf.ap().rearrange("(kc p) d -> p kc d", p=128)
    wbs_v = wbs.ap().rearrange("(kc p) d -> p kc d", p=128)

    def phase4a():
        so = S_OFF + 32768
        wgf = [AR.carve(so + i * 4096, [DC, 128], BF16) for i in range(2)]
        wgs = [AR.carve(so + 8192 + i * 4096, [DC, 128], BF16) for i in range(2)]
        wbfb = [AR.carve(so + 16384 + i * 2048, [8, 128], BF16) for i in range(2)]
        wbsb = [AR.carve(so + 20480 + i * 2048, [8, 128], BF16) for i in range(2)]
        sgf = [AR.carve(so + 24576 + i * 2048, [512], F32) for i in range(2)]
        sgs = [AR.carve(so + 28672 + i * 2048, [512], F32) for i in range(2)]
        W4 = [K.res("w4_0"), K.res("w4_1")]
        SGF = [K.res("sgf0"), K.res("sgf1")]; SGS = [K.res("sgs0"), K.res("sgs1")]

        def load(m):
            s = m % 2
            K.dma("pool", wgf[s], win_v[:, :, C_GF + m * 128:C_GF + (m + 1) * 128], writes=[W4[s]])
            K.dma("pool", wgs[s], win_v[:, :, C_GS + m * 128:C_GS + (m + 1) * 128], writes=[W4[s]])
            K.dma("pool", wbfb[s], wbf_v[:, :, m * 128:(m + 1) * 128], writes=[W4[s]])
            K.dma("pool", wbsb[s], wbs_v[:, :, m * 128:(m + 1) * 128], writes=[W4[s]])

        load(0); load(1)
        cnt = 0
        for m in range(DC):
            s = m % 2
            for ti, (t0, tn) in enumerate(TILES2):
                pb = 4 * (cnt % 2)
                j = cnt % 2
                cnt += 1
                for kc in range(8):
                    K.op("pe", lambda: nc.tensor.matmul(PS[pb][:, :], wbfb[s][:, kc, :], of[:, kc, t0:t0 + tn], start=(kc == 0), stop=(kc == 7)),
                         reads=[W4[s], OF], writes=[PSR[pb]], mark=(kc == 7))
                for c in range(DC):
                    K.op("pe", lambda: nc.tensor.matmul(PS[pb + 1][:, :], wgf[s][:, c, :], xn[:, c, t0:t0 + tn], start=(c == 0), stop=(c == DC - 1)),
                         reads=[W4[s]] + xnr(t0, tn), writes=[PSR[pb + 1]], mark=(c == DC - 1))
                for kc in range(8):
                    K.op("pe", lambda: nc.tensor.matmul(PS[pb + 2][:, :], wbsb[s][:, kc, :], osb[:, kc, t0:t0 + tn], start=(kc == 0), stop=(kc == 7)),
                         reads=[W4[s], OS], writes=[PSR[pb + 2]], mark=(kc == 7))
                for c in range(DC):
                    K.op("pe", lambda: nc.tensor.matmul(PS[pb + 3][:, :], wgs[s][:, c, :], xn[:, c, t0:t0 + tn], start=(c == 0), stop=(c == DC - 1)),
                         reads=[W4[s]] + xnr(t0, tn), writes=[PSR[pb + 3]], mark=(c == DC - 1))
                K.op("act", lambda: nc.scalar.activation(out=sgf[j], in_=PS[pb + 1][:, :], func=AF.Sigmoid), reads=[PSR[pb + 1]], writes=[SGF[j]])
                K.op("act", lambda: nc.scalar.activation(out=sgs[j], in_=PS[pb + 3][:, :], func=AF.Sigmoid), reads=[PSR[pb + 3]], writes=[SGS[j]])
                K.op("dve", lambda: nc.vector.tensor_tensor(out=sgf[j], in0=sgf[j], in1=PS[pb][:, :], op=ALU.mult), reads=[SGF[j], PSR[pb]], writes=[SGF[j]])
                K.op("dve", lambda: nc.vector.tensor_tensor(out=sgs[j], in0=sgs[j], in1=PS[pb + 2][:, :], op=ALU.mult), reads=[SGS[j], PSR[pb + 2]], writes=[SGS[j]])
                K.op("dve", lambda: nc.vector.tensor_tensor(out=merged[:, m, t0:t0 + tn], in0=sgf[j], in1=sgs[j], op=ALU.add), reads=[SGF[j], SGS[j]], writes=[MERGED])
            if m + 2 < DC:
                load(m + 2)

    phase4a()
    K.barrier()
    for c4 in range(4):
        K.dma("sp", h[:, 4 * c4:4 * c4 + 4, 0:TO], hsp_v[:, 4 * c4:4 * c4 + 4, :], reads=[HSP], writes=[HALL])
    for c in range(DC):
        for t in range(2):
            HR[c][t].w = HALL.w
            HR[c][t].r = {}

    def phase4b():
        so = S_OFF + 32768
        wos = [AR.carve(so + i * 8192, [DC, 256], BF16) for i in range(2)]
        WO = [K.res("wo0"), K.res("wo1")]
        wo_v = wo.ap().rearrange("(c p) f -> p c f", p=128)

        def load(g):
            K.dma("pool", wos[g % 2], wo_v[:, :, g * 256:(g + 1) * 256], writes=[WO[g % 2]])

        load(0); load(1)
        cnt = 0
        for g in range(8):
            s = g % 2
            for mc in range(2):
                m2 = 2 * g + mc
                for ti, (t0, tn) in enumerate(TILES2):
                    pb = cnt % 4
                    cnt += 1
                    for m in range(DC):
                        K.op("pe", lambda: nc.tensor.matmul(PS[pb][:, :], wos[s][:, m, mc * 128:(mc + 1) * 128], merged[:, m, t0:t0 + tn], start=(m == 0), stop=(m == DC - 1)),
                             reads=[WO[s], MERGED], writes=[PSR[pb]], mark=(m == DC - 1))
                    K.op("dve", lambda: nc.vector.tensor_tensor(out=h[:, m2, t0:t0 + tn], in0=PS[pb][:, :], in1=h[:, m2, t0:t0 + tn], op=ALU.add),
                         reads=[PSR[pb], HR[m2][ti]], writes=[HR[m2][ti]])
            if g + 2 < 8:
                load(g + 2)

    phase4b()
    K.barrier()

    if FULL:
        rmsnorm(g3_sb, TILES2, S_OFF + 61568)
        ffn(w2g, w2u, w2d, TILES2, S_OFF)
        K.barrier()

    OUT = K.res("out")
    outT_v = outT.ap().rearrange("(c p) t -> p c t", p=128)
    for c4 in range(4):
        K.dma("sp", outT_v[:, 4 * c4:4 * c4 + 4, :], h[:, 4 * c4:4 * c4 + 4, 0:TO], writes=[OUT])
    K.wait("sp", OUT.w)
    return nc


def _bf(a):
    return np.ascontiguousarray(a).astype(ml_dtypes.bfloat16)


def make_consts():
    ones = np.ones((128, 128), np.float32)
    ident = np.eye(128, dtype=np.float32)
    sp = np.arange(128)[:, None]; s = np.arange(128)[None, :]
    tri = (sp >= s).astype(np.float32)
    tric = (sp < s).astype(np.float32)
    cmat = np.stack([ones, ident, tri, tric], axis=1)
    p = np.arange(128)[:, None]; f = np.arange(512)[None, :]
    diag = np.zeros((128, 8, 512), np.float32)
    for kb in range(4):
        diag[:, kb, :] = np.where(kb * 128 + p <= f, 0.0, NEG)
        diag[:, 4 + kb, :] = np.where(kb * 128 + p < f, 0.0, NEG)
    qsel = np.zeros((2, TO), np.float32)
    qsel[0, :512] = 1.0
    qsel[1, 512:] = 1.0
    return dict(cmat=_bf(cmat), cones=ones, cdiag=_bf(diag), cqsel=_bf(qsel))


def core_consts(j):
    A, B = j, 7 - j
    isel = np.zeros((128, 8, 128), np.float32)
    isel[:, A, :] = np.eye(128)
    isel[:, 4 + (B - 4), :] = np.eye(128)
    kmask = np.zeros((2, LSEQ), np.float32)
    kmask[0, NM + (A + 1) * 512:] = NEG
    kmask[1, NM + (B + 1) * 512:] = NEG
    presel = np.zeros((8, 2, 9), np.float32)
    presel[:, 0, A] = 1.0
    presel[:, 1, B] = 1.0
    return dict(cisel=_bf(isel), ckmask=_bf(kmask), cpresel=presel)


_NC_CACHE = {}


def kernel(x, meta_tokens, ffn1_norm, ffn1_w_gate, ffn1_w_up, ffn1_w_down, mix_norm, w_in, b_forget,
           fox_q_norm, fox_k_norm, w_branch_fox, w_branch_sb, w_out, ffn2_norm, ffn2_w_gate, ffn2_w_up,
           ffn2_w_down, _mode="full", _h1=None, _ncores=8):
    f32 = lambda a: np.ascontiguousarray(np.asarray(a, dtype=np.float32))
    x = f32(x); meta = f32(meta_tokens)
    gl = lambda g: f32(np.asarray(g)[0].reshape(DC, 128).T)
    common = dict(
        g1=gl(ffn1_norm), g2=gl(mix_norm), g3=gl(ffn2_norm),
        gq=f32(np.asarray(fox_q_norm)[0].T), gk=f32(np.asarray(fox_k_norm)[0].T),
        bfo=f32(np.asarray(b_forget)[0].reshape(8, 1)),
        w1g=f32(np.asarray(ffn1_w_gate)[0]), w1u=f32(np.asarray(ffn1_w_up)[0]), w1d=f32(np.asarray(ffn1_w_down)[0]),
        w2g=f32(np.asarray(ffn2_w_gate)[0]), w2u=f32(np.asarray(ffn2_w_up)[0]), w2d=f32(np.asarray(ffn2_w_down)[0]),
        win=f32(np.asarray(w_in)[0]), wbf=f32(np.asarray(w_branch_fox)[0]), wbs=f32(np.asarray(w_branch_sb)[0]),
        wo=f32(np.asarray(w_out)[0]),
    )
    common.update(make_consts())
    in_maps = []
    for core in range(_ncores):
        b, j = core // 4, core % 4
        A, B = j, 7 - j
        if _h1 is None:
            own = np.concatenate([x[b, A * 512:(A + 1) * 512], x[b, B * 512:(B + 1) * 512], meta], axis=0)
        else:
            hb = _h1[b]
            own = np.concatenate([hb[16 + A * 512:16 + (A + 1) * 512], hb[16 + B * 512:16 + (B + 1) * 512], hb[0:16]], axis=0)
        m = dict(common)
        m["xT"] = np.ascontiguousarray(own.T)
        m.update(core_consts(j))
        in_maps.append(m)
    key = (_mode, _ncores)
    if key not in _NC_CACHE:
        _NC_CACHE[key] = build(_mode, _ncores // 4)
    nc = _NC_CACHE[key]
    if _mode != "full":
        for m in in_maps:
            for k in ("w1g", "w1u", "w1d", "w2g", "w2u", "w2d"):
                m.pop(k, None)
    res = run_bass_kernel_spmd(nc, in_maps, core_ids=list(range(_ncores)))
    out = np.zeros((2, 4096, D), np.float32)
    for core in range(_ncores):
        b, j = core // 4, core % 4
        A, B = j, 7 - j
        o = np.asarray(res.results[core]["outT"]).T
        out[b, A * 512:(A + 1) * 512] = o[:512]
        out[b, B * 512:(B + 1) * 512] = o[512:]
    return out
```

```python
import numpy as np
import ml_dtypes
import concourse.bass as bass
import concourse.mybir as mybir
from concourse.bass_utils import run_bass_kernel_spmd

F32 = mybir.dt.float32
BF16 = mybir.dt.bfloat16
AF = mybir.ActivationFunctionType
ALU = mybir.AluOpType
AX = mybir.AxisListType

D = 2048
DC = 16
DFF = 5632
NG = DFF // 256
TO = 1024
TA = 1040
NM = 16
LSEQ = 4112
EPS = 1e-6
NEG = -30000.0
INW = 10248
C_FQ, C_FK, C_FV, C_F, C_SQ, C_SK, C_SV, C_GF, C_GS = 0, 1024, 2048, 3072, 3080, 4104, 5128, 6152, 8200


class Res:
    def __init__(self, name=""):
        self.name = name
        self.w = None
        self.r = {}
        self.dsem = None
        self.dcnt = 0


class Ctx:
    def __init__(self, nc):
        self.nc = nc
        self.E = {"pe": nc.tensor, "act": nc.scalar, "dve": nc.vector, "pool": nc.gpsimd, "sp": nc.sync}
        self.esem = {e: nc.alloc_semaphore(name="es_" + e) for e in ("pe", "act", "dve", "pool")}
        self.ecnt = {e: 0 for e in self.esem}
        self.seen = {}
        self.all_res = []
        self.semkey = {}
        self.cc_sem = nc.alloc_semaphore(name="cc_sem")
        self.cc_cnt = 0

    def res(self, name=""):
        r = Res(name)
        self.all_res.append(r)
        return r

    def _k(self, sem):
        return id(sem)

    def wait(self, eng, tok):
        if tok is None:
            return
        sem, val = tok
        if eng == "pe" and sem is self.esem["pe"]:
            return
        key = (eng, self._k(sem))
        if self.seen.get(key, 0) >= val:
            return
        self.seen[key] = val
        self.E[eng].wait_ge(sem, val)

    def mark(self, eng, ins):
        self.ecnt[eng] += 1
        ins.then_inc(self.esem[eng], 1)
        return (self.esem[eng], self.ecnt[eng])

    def pre(self, eng, reads, writes):
        for r in reads:
            self.wait(eng, r.w)
        for w in writes:
            self.wait(eng, w.w)
            for t in list(w.r.values()):
                self.wait(eng, t)

    def post(self, tok, reads, writes):
        for r in reads:
            r.r[self._k(tok[0])] = tok
        for w in writes:
            w.w = tok
            w.r = {}

    def op(self, eng, fn, reads=(), writes=(), mark=True):
        self.pre(eng, reads, writes)
        ins = fn()
        if mark:
            tok = self.mark(eng, ins)
            self.post(tok, reads, writes)
            return tok
        return None

    def dma(self, q, out, in_, reads=(), writes=(), **kw):
        self.pre(q, reads, writes)
        res = writes[0]
        if res.dsem is None:
            self.nsem = getattr(self, "nsem", 0) + 1
            res.dsem = self.nc.alloc_semaphore(name="ds%d_%s" % (self.nsem, res.name))
        res.dcnt += 16
        self.E[q].dma_start(out=out, in_=in_, **kw).then_inc(res.dsem, 16)
        tok = (res.dsem, res.dcnt)
        self.post(tok, reads, writes)
        return tok

    def collective(self, ins_ap, outs_ap, reads, writes, groups):
        self.pre("pool", reads, writes)
        self.cc_cnt += 1
        self.nc.gpsimd.collective_compute(
            "AllGather", ALU.bypass, replica_groups=groups, ins=[ins_ap], outs=[outs_ap], dma_qos="P3"
        ).then_inc(self.cc_sem, 1)
        tok = (self.cc_sem, self.cc_cnt)
        self.post(tok, reads, writes)
        return tok

    def barrier(self, skip_cc=False):
        toks = [(self.esem[e], self.ecnt[e]) for e in self.esem if self.ecnt[e] > 0]
        for r in self.all_res:
            if r.dsem is not None and r.dcnt > 0:
                toks.append((r.dsem, r.dcnt))
        if self.cc_cnt and not skip_cc:
            toks.append((self.cc_sem, self.cc_cnt))
        for eng in ("pe", "act", "dve", "pool", "sp"):
            for t in toks:
                self.wait(eng, t)


class Arena:
    def __init__(self, tens, nbytes):
        self.t = tens
        self.nbytes = nbytes

    def carve(self, off, shape, dt, parts=128):
        esz = 4 if dt == F32 else 2
        n = int(np.prod(shape)) * esz
        assert off % 4 == 0 and off + n <= self.nbytes, (off, n, self.nbytes)
        a = self.t[0:parts, off // 2:(off + n) // 2]
        if dt == F32:
            a = a.bitcast(F32)
        if len(shape) == 2:
            a = a.rearrange("p (a b) -> p a b", a=shape[0])
        elif len(shape) == 3:
            a = a.rearrange("p (a b c) -> p a b c", a=shape[0], b=shape[1])
        return a


ARENA_BYTES = 212000


def build(mode="full", ngroups=2):
    nc = bass.Bass("TRN2", target_bir_lowering=False)
    K = Ctx(nc)

    def din(name, shape, dt=F32):
        return nc.dram_tensor(name, list(shape), dt, kind="ExternalInput")

    FULL = (mode == "full")
    GROUPS = [[0, 1, 2, 3], [4, 5, 6, 7]][:ngroups]

    xT = din("xT", [D, TA])
    g1 = din("g1", [128, DC]); g2 = din("g2", [128, DC]); g3 = din("g3", [128, DC])
    gq = din("gq", [128, 8]); gk = din("gk", [128, 8]); bfo = din("bfo", [8, 1])
    if FULL:
        w1g = din("w1g", [D, DFF]); w1u = din("w1u", [D, DFF]); w1d = din("w1d", [DFF, D])
        w2g = din("w2g", [D, DFF]); w2u = din("w2u", [D, DFF]); w2d = din("w2d", [DFF, D])
    win = din("win", [D, INW]); wbf = din("wbf", [1024, D]); wbs = din("wbs", [1024, D]); wo = din("wo", [D, D])
    cmat = din("cmat", [128, 4, 128], BF16)
    cones = din("cones", [128, 128], F32)
    cdiag = din("cdiag", [128, 8, 512], BF16)
    cisel = din("cisel", [128, 8, 128], BF16)
    ckmask = din("ckmask", [2, LSEQ], BF16)
    cqsel = din("cqsel", [2, TO], BF16)
    cpresel = din("cpresel", [8, 2, 9], F32)
    outT = nc.dram_tensor("outT", [D, TO], F32, kind="ExternalOutput")
    hspill = nc.dram_tensor("hspill", [D, TO], F32)

    arena_cm = nc.sbuf_tensor("arena", [128, ARENA_BYTES // 2], BF16)
    arena_t = arena_cm.__enter__()
    AR = Arena(arena_t, ARENA_BYTES)
    ps_cms = [nc.psum_tensor("ps%d" % i, [128, 512], F32) for i in range(8)]
    PS = [cm.__enter__() for cm in ps_cms]
    PSR = [K.res("ps%d" % i) for i in range(8)]

    o = 0
    def alloc(shape, dt, parts=128):
        nonlocal o
        esz = 4 if dt == F32 else 2
        a = AR.carve(o, shape, dt, parts)
        o += (int(np.prod(shape)) * esz + 3) // 4 * 4
        return a
    ones_f = alloc([128], F32)
    cm_sb = alloc([4, 128], BF16)
    ones_b, ident_b, tri_b, tric_b = cm_sb[:, 0, :], cm_sb[:, 1, :], cm_sb[:, 2, :], cm_sb[:, 3, :]
    g1_sb = alloc([DC], F32); g2_sb = alloc([DC], F32); g3_sb = alloc([DC], F32)
    gq_sb = alloc([8], F32); gk_sb = alloc([8], F32); bfo_sb = alloc([1], F32); nb_sb = alloc([1], F32); gqs_sb = alloc([8], F32)
    diag_sb = alloc([8, 512], BF16)
    isel_sb = alloc([8, 128], BF16)
    kmask_sb = alloc([LSEQ], BF16)
    qsel_sb = alloc([TO], BF16)
    presel_sb = alloc([2, 9], F32)
    metaKf = alloc([8, NM], BF16); metaKs = alloc([8, NM], BF16)
    metaVf = alloc([1024], BF16); metaVs = alloc([1024], BF16)
    sp_own = alloc([TA], F32)
    xn = alloc([DC, TA], BF16)
    G_END = o
    H_OFF = G_END
    h = AR.carve(H_OFF, [DC, TA], F32)
    S_OFF = H_OFF + DC * TA * 4
    CONST = K.res("const")
    XNS = {"tiles": [], "res": []}

    def xnr(t0, tn):
        return [r for (a, n_), r in zip(XNS["tiles"], XNS["res"]) if a < t0 + tn and t0 < a + n_]
    HR = [[K.res("h%d_%d" % (c, t)) for t in range(3)] for c in range(DC)]

    C_A = K.res("constA")
    K.dma("sp", ones_f, cones.ap(), writes=[C_A])
    K.dma("sp", g1_sb, g1.ap(), writes=[C_A])
    TILES3 = [(0, 352), (352, 352), (704, 336)]
    TILES2 = [(0, 512), (512, 512)]
    xT_v = xT.ap().rearrange("(c p) t -> p c t", p=128)
    HT = [K.res("ht%d" % i) for i in range(3)]
    for ti, (t0, tn) in enumerate(TILES3):
        K.dma("sp", h[:, :, t0:t0 + tn], xT_v[:, :, t0:t0 + tn], writes=[HT[ti]])
        for c in range(DC):
            HR[c][ti].w = HT[ti].w
    C_B = K.res("constB")
    K.dma("sp", cm_sb, cmat.ap(), writes=[C_B])
    for sb, dr in ((g2_sb, g2), (g3_sb, g3), (gq_sb, gq), (gk_sb, gk)):
        K.dma("sp", sb, dr.ap(), writes=[C_B])
    K.dma("sp", bfo_sb[0:8, :], bfo.ap(), writes=[C_B])
    K.dma("sp", diag_sb, cdiag.ap(), writes=[CONST])
    K.dma("sp", isel_sb, cisel.ap(), writes=[CONST])
    K.op("dve", lambda: nc.vector.memset(kmask_sb[:, :], 0.0), writes=[CONST])
    K.op("dve", lambda: nc.vector.memset(qsel_sb[:, :], 0.0), writes=[CONST])
    K.dma("sp", kmask_sb[0:2, :], ckmask.ap(), writes=[CONST])
    K.dma("sp", qsel_sb[0:2, :], cqsel.ap(), writes=[CONST])
    K.dma("sp", presel_sb[0:8, :, :], cpresel.ap(), writes=[CONST])
    HALL = K.res("hall")

    def rmsnorm(gain_sb, tiles, soff):
        sq = [AR.carve(soff + i * 2048, [512], F32) for i in range(2)]
        lnv = AR.carve(soff + 4096, [512], F32)
        rstd = AR.carve(soff + 6144, [512], F32)
        SQ = [K.res("sq0"), K.res("sq1")]
        LNV = K.res("lnv"); RSTD = K.res("rstd")
        XNS["tiles"] = list(tiles)
        XNS["res"] = [K.res("xn%d" % i) for i in range(len(tiles))]
        for ti, (t0, tn) in enumerate(tiles):
            pb = 6 + (ti % 2)
            for c in range(DC):
                K.op("act", lambda: nc.scalar.activation(out=sq[c % 2][:, 0:tn], in_=h[:, c, t0:t0 + tn], func=AF.Square),
                     reads=[HR[c][ti]], writes=[SQ[c % 2]])
                K.op("pe", lambda: nc.tensor.matmul(PS[pb][:, 0:tn], ones_f, sq[c % 2][:, 0:tn], start=(c == 0), stop=(c == DC - 1)),
                     reads=[SQ[c % 2], C_A], writes=[PSR[pb]])
            K.op("act", lambda: nc.scalar.activation(out=lnv[:, 0:tn], in_=PS[pb][:, 0:tn], func=AF.Ln, scale=1.0 / D, bias=EPS),
                 reads=[PSR[pb]], writes=[LNV])
            K.op("act", lambda: nc.scalar.activation(out=rstd[:, 0:tn], in_=lnv[:, 0:tn], func=AF.Exp, scale=-0.5),
                 reads=[LNV], writes=[RSTD])
            for c in range(DC):
                K.op("dve", lambda: nc.vector.scalar_tensor_tensor(out=xn[:, c, t0:t0 + tn], in0=h[:, c, t0:t0 + tn], scalar=gain_sb[:, c:c + 1],
                                                                   in1=rstd[:, 0:tn], op0=ALU.mult, op1=ALU.mult),
                     reads=[HR[c][ti], RSTD, C_A], writes=[XNS["res"][ti]])

    def ffn(wg, wu, wd, tiles, soff):
        wA = [AR.carve(soff + i * 8192, [DC, 256], BF16) for i in range(2)]
        wB = [AR.carve(soff + 16384 + i * 8192, [DC, 256], BF16) for i in range(2)]
        wD = [AR.carve(soff + 32768 + i * 8192, [2, D], BF16) for i in range(2)]
        act = [AR.carve(soff + 49152 + i * 4160, [2, TA], BF16) for i in range(2)]
        sg = [AR.carve(soff + 57472 + i * 2048, [512], F32) for i in range(2)]
        WA = [K.res("wA0"), K.res("wA1")]; WB = [K.res("wB0"), K.res("wB1")]; WD = [K.res("wD0"), K.res("wD1")]
        ACT = [K.res("act0"), K.res("act1")]; SG = [K.res("sg0"), K.res("sg1")]
        wg_v = wg.ap().rearrange("(c p) f -> p c f", p=128)
        wu_v = wu.ap().rearrange("(c p) f -> p c f", p=128)

        def load_ab(g):
            s = g % 2
            K.dma("pool", wA[s], wg_v[:, :, g * 256:(g + 1) * 256], writes=[WA[s]])
            K.dma("pool", wB[s], wu_v[:, :, g * 256:(g + 1) * 256], writes=[WB[s]])

        def load_d(g):
            s = g % 2
            K.dma("pool", wD[s], wd.ap()[g * 256:(g + 1) * 256, :].rearrange("(fc p) d -> p fc d", p=128), writes=[WD[s]])

        cnt = [0]

        def part_a(g):
            s = g % 2
            for fc in range(2):
                for ti, (t0, tn) in enumerate(tiles):
                    i = cnt[0] % 2
                    cnt[0] += 1
                    pg, pu = i, 2 + i
                    for c in range(DC):
                        K.op("pe", lambda: nc.tensor.matmul(PS[pg][:, 0:tn], wA[s][:, c, fc * 128:(fc + 1) * 128], xn[:, c, t0:t0 + tn], start=(c == 0), stop=(c == DC - 1)),
                             reads=[WA[s]] + xnr(t0, tn), writes=[PSR[pg]], mark=(c == DC - 1))
                    yield
                    for c in range(DC):
                        K.op("pe", lambda: nc.tensor.matmul(PS[pu][:, 0:tn], wB[s][:, c, fc * 128:(fc + 1) * 128], xn[:, c, t0:t0 + tn], start=(c == 0), stop=(c == DC - 1)),
                             reads=[WB[s]] + xnr(t0, tn), writes=[PSR[pu]], mark=(c == DC - 1))
                    K.op("act", lambda: nc.scalar.activation(out=sg[i][:, 0:tn], in_=PS[pg][:, 0:tn], func=AF.Silu), reads=[PSR[pg]], writes=[SG[i]])
                    K.op("dve", lambda: nc.vector.tensor_tensor(out=act[s][:, fc, t0:t0 + tn], in0=sg[i][:, 0:tn], in1=PS[pu][:, 0:tn], op=ALU.mult),
                         reads=[SG[i], PSR[pu]], writes=[ACT[s]])
                    yield

        dcnt = [0]

        def part_b(g):
            s = g % 2
            for dc in range(DC):
                for ti, (t0, tn) in enumerate(tiles):
                    pd = 4 + dcnt[0] % 4
                    dcnt[0] += 1
                    for fc in range(2):
                        K.op("pe", lambda: nc.tensor.matmul(PS[pd][:, 0:tn], wD[s][:, fc, dc * 128:(dc + 1) * 128], act[s][:, fc, t0:t0 + tn], start=(fc == 0), stop=(fc == 1)),
                             reads=[WD[s], ACT[s]], writes=[PSR[pd]], mark=(fc == 1))
                    K.op("dve", lambda: nc.vector.scalar_tensor_tensor(out=h[:, dc, t0:t0 + tn], in0=PS[pd][:, 0:tn], scalar=0.5, in1=h[:, dc, t0:t0 + tn], op0=ALU.mult, op1=ALU.add),
                         reads=[PSR[pd]], writes=[HR[dc][ti]])
                    yield

        def run(gen):
            for _ in gen:
                pass

        load_ab(0); load_d(0); load_ab(1); load_d(1)
        run(part_a(0))
        for g in range(NG):
            gb = part_b(g)
            if g + 1 < NG:
                for _ in part_a(g + 1):
                    for _k in range(4):
                        next(gb, None)
            run(gb)
            if g + 2 < NG:
                load_ab(g + 2)
                load_d(g + 2)

    QSCALE = 128.0 ** -0.5

    if FULL:
        rmsnorm(g1_sb, TILES3, S_OFF + 61568)
        ffn(w1g, w1u, w1d, TILES3, S_OFF)
        K.barrier()

    K.op("dve", lambda: nc.vector.tensor_scalar(out=nb_sb[0:8, :], in0=bfo_sb[0:8, :], scalar1=-1.0, scalar2=None, op0=ALU.mult),
         reads=[C_B], writes=[CONST])
    K.op("dve", lambda: nc.vector.tensor_scalar(out=gqs_sb, in0=gq_sb, scalar1=QSCALE, scalar2=None, op0=ALU.mult),
         reads=[C_B], writes=[CONST])
    rmsnorm(g2_sb, TILES3, S_OFF + 61504)
    HSP = K.res("hspill")
    hsp_v = hspill.ap().rearrange("(c p) t -> p c t", p=128)
    for c4 in range(4):
        K.dma("sp", hsp_v[:, 4 * c4:4 * c4 + 4, :], h[:, 4 * c4:4 * c4 + 4, 0:TO],
              reads=[HR[c][t] for c in range(4 * c4, 4 * c4 + 4) for t in range(3)], writes=[HSP])
    K.barrier()

    qf = AR.carve(H_OFF, [8, TO], BF16); qs = AR.carve(H_OFF + 16384, [8, TO], BF16)
    of = AR.carve(H_OFF + 32768, [8, TO], BF16); osb = AR.carve(H_OFF + 49152, [8, TO], BF16)
    QF = K.res("qf"); QS = K.res("qs"); OF = K.res("of"); OS = K.res("os")
    METAR = K.res("meta")
    SPO = K.res("spown")

    agin_k = {t: [nc.dram_tensor("agin_k%s%d" % (t, p), [1024, 512], BF16) for p in range(2)] for t in "fs"}
    agin_v = {t: [nc.dram_tensor("agin_v%s%d" % (t, p), [512, 1024], BF16) for p in range(2)] for t in "fs"}
    agout_k = {t: [nc.dram_tensor("agout_k%s%d" % (t, p), [4 * 1024, 512], BF16) for p in range(2)] for t in "fs"}
    agout_v = {t: [nc.dram_tensor("agout_v%s%d" % (t, p), [4 * 512, 1024], BF16) for p in range(2)] for t in "fs"}
    agin_f = nc.dram_tensor("agin_f", [8, TO], F32)
    agout_f = nc.dram_tensor("agout_f", [32, TO], F32)
    AGIN_K = {t: [K.res("agink%s%d" % (t, p)) for p in range(2)] for t in "fs"}
    AGIN_V = {t: [K.res("aginv%s%d" % (t, p)) for p in range(2)] for t in "fs"}
    AGOUT_K = {t: [K.res("agoutk%s%d" % (t, p)) for p in range(2)] for t in "fs"}
    AGOUT_V = {t: [K.res("agoutv%s%d" % (t, p)) for p in range(2)] for t in "fs"}
    AGIN_F = K.res("aginf"); AGOUT_F = K.res("agoutf")

    win_v = win.ap().rearrange("(c p) f -> p c f", p=128)

    def phase2():
        so = S_OFF
        wA = [AR.carve(so + i * 8192, [DC, 256], BF16) for i in range(2)]
        stg = [AR.carve(so + 16384 + i * 16384, [DC, 256], F32) for i in range(2)]
        kst = [AR.carve(so + 49152 + i * 2080, [TA], BF16) for i in range(2)]
        vst = [AR.carve(so + 53312 + i * 4096, [8, 256], BF16) for i in range(2)]
        sqb = [AR.carve(so + 61504 + i * 2048, [512], F32) for i in range(2)]
        lnv = AR.carve(so + 65600, [512], F32)
        rstd = AR.carve(so + 67648, [512], F32)
        fE = AR.carve(so + 69696, [TA], F32)
        WA = [K.res("p2wA%d" % i) for i in range(2)]
        STG = [K.res("p2stg%d" % i) for i in range(2)]
        KST = [K.res("kst0"), K.res("kst1")]; VST = [K.res("vst0"), K.res("vst1")]
        SQB = [K.res("sqb0"), K.res("sqb1")]; LNV = K.res("p2lnv"); RSTD = K.res("p2rstd"); FE = K.res("fE")

        groups = [("f", C_F, 0, 8)]
        for i in range(4):
            groups.append(("kf", C_FK + 256 * i, i, 256))
        for i in range(4):
            groups.append(("vf", C_FV + 256 * i, i, 256))
        for i in range(4):
            groups.append(("ks", C_SK + 256 * i, i, 256))
        for i in range(4):
            groups.append(("vs", C_SV + 256 * i, i, 256))
        n_kv = len(groups)
        for i in range(4):
            groups.append(("qf", C_FQ + 256 * i, i, 256))
        for i in range(4):
            groups.append(("qs", C_SQ + 256 * i, i, 256))

        def load(gi):
            kind, c0, i, ncol = groups[gi]
            K.dma("sp", stg[gi % 2][:, :, 0:ncol], win_v[:, :, c0:c0 + ncol], writes=[STG[gi % 2]])

        def convert(gi):
            kind, c0, i, ncol = groups[gi]
            K.op("dve", lambda: nc.vector.tensor_copy(out=wA[gi % 2][:, :, 0:ncol], in_=stg[gi % 2][:, :, 0:ncol]), reads=[STG[gi % 2]], writes=[WA[gi % 2]])

        rot = [0]
        hcnt = [0]
        vcnt = [0]

        def normed(pk, pidx, tn, gain_ap, dest, dres):
            i2 = rot[0] % 2
            K.op("act", lambda: nc.scalar.activation(out=sqb[i2][:, 0:tn], in_=PS[pidx][:, 0:tn], func=AF.Square), reads=[PSR[pidx]], writes=[SQB[i2]])
            K.op("pe", lambda: nc.tensor.matmul(PS[6 + i2][:, 0:tn], ones_f, sqb[i2][:, 0:tn], start=True, stop=True), reads=[SQB[i2], CONST], writes=[PSR[6 + i2]])
            K.op("act", lambda: nc.scalar.activation(out=lnv[:, 0:tn], in_=PS[6 + i2][:, 0:tn], func=AF.Ln, scale=1.0 / 128, bias=EPS), reads=[PSR[6 + i2]], writes=[LNV])
            K.op("act", lambda: nc.scalar.activation(out=rstd[:, 0:tn], in_=lnv[:, 0:tn], func=AF.Exp, scale=-0.5), reads=[LNV], writes=[RSTD])
            K.op("dve", lambda: nc.vector.scalar_tensor_tensor(out=dest, in0=PS[pidx][:, 0:tn], scalar=gain_ap, in1=rstd[:, 0:tn], op0=ALU.mult, op1=ALU.mult),
                 reads=[PSR[pidx], RSTD, CONST], writes=[dres])

        def compute(gi):
            kind, c0, i, ncol = groups[gi]
            s = gi % 2
            if kind == "f":
                for ti, (t0, tn) in enumerate(TILES3):
                    for c in range(DC):
                        K.op("pe", lambda: nc.tensor.matmul(PS[4][0:8, 0:tn], wA[s][:, c, 0:8], xn[:, c, t0:t0 + tn], start=(c == 0), stop=(c == DC - 1)),
                             reads=[WA[s]] + xnr(t0, tn), writes=[PSR[4]], mark=(c == DC - 1))
                    K.op("act", lambda: nc.scalar.activation(out=fE[0:8, t0:t0 + tn], in_=PS[4][0:8, 0:tn], func=AF.Exp, scale=-1.0, bias=nb_sb[0:8, :]),
                         reads=[PSR[4], CONST], writes=[FE])
                    K.op("act", lambda: nc.scalar.activation(out=sp_own[0:8, t0:t0 + tn], in_=fE[0:8, t0:t0 + tn], func=AF.Ln, bias=1.0),
                         reads=[FE], writes=[SPO])
                K.dma("pool", agin_f.ap(), sp_own[0:8, 0:TO], reads=[SPO], writes=[AGIN_F])
            elif kind in ("kf", "ks", "qf", "qs"):
                typ = kind[1]
                isq = kind[0] == "q"
                tiles = TILES2 if isq else TILES3
                for hh in range(2):
                    head = 2 * i + hh
                    hb = hcnt[0] % 2
                    hcnt[0] += 1
                    for ti, (t0, tn) in enumerate(tiles):
                        pidx = rot[0] % 4
                        rot[0] += 1
                        for c in range(DC):
                            K.op("pe", lambda: nc.tensor.matmul(PS[pidx][:, 0:tn], wA[s][:, c, hh * 128:(hh + 1) * 128], xn[:, c, t0:t0 + tn], start=(c == 0), stop=(c == DC - 1)),
                                 reads=[WA[s]] + xnr(t0, tn), writes=[PSR[pidx]], mark=(c == DC - 1))
                        if isq:
                            dest = (qf if typ == "f" else qs)[:, head, t0:t0 + tn]
                            dres = QF if typ == "f" else QS
                        else:
                            dest = kst[hb][:, t0:t0 + tn]
                            dres = KST[hb]
                        if typ == "f":
                            gain_ap = (gqs_sb if isq else gk_sb)[:, head:head + 1]
                            normed(PS[pidx], pidx, tn, gain_ap, dest, dres)
                        elif isq:
                            K.op("dve", lambda: nc.vector.tensor_scalar(out=dest, in0=PS[pidx][:, 0:tn], scalar1=QSCALE, scalar2=None, op0=ALU.mult),
                                 reads=[PSR[pidx]], writes=[dres])
                        else:
                            K.op("dve", lambda: nc.vector.tensor_copy(out=dest, in_=PS[pidx][:, 0:tn]), reads=[PSR[pidx]], writes=[dres])
                    if not isq:
                        mk = metaKf if typ == "f" else metaKs
                        K.op("dve", lambda: nc.vector.tensor_copy(out=mk[:, head, :], in_=kst[hb][:, TO:TA]), reads=[KST[hb]], writes=[METAR])
                        for pos in range(2):
                            K.dma("pool", agin_k[typ][pos].ap()[head * 128:(head + 1) * 128, :], kst[hb][:, pos * 512:(pos + 1) * 512],
                                  reads=[KST[hb]], writes=[AGIN_K[typ][pos]])
            else:
                typ = kind[1]
                vb = vcnt[0] % 2
                vcnt[0] += 1
                mv = metaVf if typ == "f" else metaVs
                for tb in range(9):
                    M = 128 if tb < 8 else NM
                    pidx = rot[0] % 4
                    rot[0] += 1
                    for c in range(DC):
                        K.op("pe", lambda: nc.tensor.matmul(PS[pidx][0:M, 0:256], xn[:, c, tb * 128:tb * 128 + M], wA[s][:, c, 0:256], start=(c == 0), stop=(c == DC - 1)),
                             reads=[WA[s]] + xnr(tb * 128, M), writes=[PSR[pidx]], mark=(c == DC - 1))
                    if tb < 8:
                        K.op("act", lambda: nc.scalar.copy(out=vst[vb][:, tb, :], in_=PS[pidx][:, 0:256]), reads=[PSR[pidx]], writes=[VST[vb]])
                    else:
                        K.op("act", lambda: nc.scalar.copy(out=mv[0:NM, i * 256:(i + 1) * 256], in_=PS[pidx][0:NM, 0:256]), reads=[PSR[pidx]], writes=[METAR])
                for pos in range(2):
                    K.dma("pool", agin_v[typ][pos].ap()[:, i * 256:(i + 1) * 256].rearrange("(tb p) f -> p tb f", p=128), vst[vb][:, 4 * pos:4 * pos + 4, :],
                          reads=[VST[vb]], writes=[AGIN_V[typ][pos]])

        def collectives(gi):
            kind, c0, i, ncol = groups[gi]
            if kind == "f":
                K.collective(agin_f.ap().opt(), agout_f.ap().opt(), [AGIN_F], [AGOUT_F], GROUPS)
            elif i == 3 and kind[0] in "kv":
                typ = kind[1]
                for pos in range(2):
                    if kind[0] == "k":
                        K.collective(agin_k[typ][pos].ap().opt(), agout_k[typ][pos].ap().opt(), [AGIN_K[typ][pos]], [AGOUT_K[typ][pos]], GROUPS)
                    else:
                        K.collective(agin_v[typ][pos].ap().opt(), agout_v[typ][pos].ap().opt(), [AGIN_V[typ][pos]], [AGOUT_V[typ][pos]], GROUPS)

        load(0); load(1); convert(0)
        for gi in range(len(groups)):
            if gi + 1 < len(groups):
                convert(gi + 1)
            compute(gi)
            if gi + 2 < len(groups):
                load(gi + 2)
            collectives(gi)

    phase2()
    K.barrier(skip_cc=True)

    cs_k = nc.dram_tensor("cs_k", [3, 8, LSEQ], BF16)
    cs_q = nc.dram_tensor("cs_q", [3, 8, TO], BF16)
    CSK = K.res("csk"); CSQ = K.res("csq")

    def chunk_src(ch):
        return (ch, 0) if ch < 4 else (7 - ch, 1)

    def phase25():
        so = S_OFF
        spg = AR.carve(so, [LSEQ], F32)
        Cg = AR.carve(so + 16448, [LSEQ], F32)
        kp = [AR.carve(so + 32896 + i * 8224, [LSEQ], BF16) for i in range(3)]
        Cown = AR.carve(so + 57568, [TO], F32)
        qp = [AR.carve(so + 61664 + i * 2048, [TO], BF16) for i in range(3)]
        sbv = AR.carve(so + 67808, [2], F32)
        SPG = K.res("spg"); CG = K.res("cg"); KP = K.res("kp"); COWN = K.res("cown"); QP = K.res("qp"); SBV = K.res("sbv")
        K.dma("sp", spg[0:8, 0:NM], sp_own[0:8, TO:TA], reads=[SPO], writes=[SPG])
        for ch in range(8):
            rank, pos = chunk_src(ch)
            K.dma("sp", spg[0:8, NM + ch * 512:NM + (ch + 1) * 512], agout_f.ap()[rank * 8:rank * 8 + 8, pos * 512:(pos + 1) * 512],
                  reads=[AGOUT_F], writes=[SPG])
        K.op("dve", lambda: nc.vector.tensor_tensor_scan(out=Cg[0:8, :], data0=spg[0:8, :], data1=spg[0:8, :], initial=0.0, op0=ALU.add, op1=ALU.bypass),
             reads=[SPG], writes=[CG])
        K.op("dve", lambda: nc.vector.memset(sbv[0:8, :], 0.0), writes=[SBV])
        for qi in range(2):
            for k in range(9):
                pos = NM - 1 + 512 * k
                K.op("dve", lambda: nc.vector.scalar_tensor_tensor(out=sbv[0:8, qi:qi + 1], in0=Cg[0:8, pos:pos + 1], scalar=presel_sb[0:8, qi, k:k + 1],
                                                                   in1=sbv[0:8, qi:qi + 1], op0=ALU.mult, op1=ALU.add),
                     reads=[CG, CONST, SBV], writes=[SBV])
        for qi in range(2):
            K.op("dve", lambda: nc.vector.tensor_tensor_scan(out=Cown[0:8, qi * 512:(qi + 1) * 512], data0=sp_own[0:8, qi * 512:(qi + 1) * 512],
                                                            data1=sp_own[0:8, qi * 512:(qi + 1) * 512], initial=sbv[0:8, qi:qi + 1], op0=ALU.add, op1=ALU.bypass),
                 reads=[SPO, SBV], writes=[COWN])
        K.op("dve", lambda: nc.vector.tensor_copy(out=kp[0][0:8, :], in_=Cg[0:8, :]), reads=[CG], writes=[KP])
        K.op("dve", lambda: nc.vector.tensor_tensor(out=spg[0:8, :], in0=Cg[0:8, :], in1=kp[0][0:8, :], op=ALU.subtract), reads=[CG, KP], writes=[SPG])
        K.op("dve", lambda: nc.vector.tensor_copy(out=kp[1][0:8, :], in_=spg[0:8, :]), reads=[SPG], writes=[KP])
        K.op("dve", lambda: nc.vector.tensor_tensor(out=spg[0:8, :], in0=spg[0:8, :], in1=kp[1][0:8, :], op=ALU.subtract), reads=[SPG, KP], writes=[SPG])
        K.op("dve", lambda: nc.vector.tensor_copy(out=kp[2][0:8, :], in_=spg[0:8, :]), reads=[SPG], writes=[KP])
        K.op("dve", lambda: nc.vector.tensor_scalar(out=Cown[0:8, :], in0=Cown[0:8, :], scalar1=-1.0, scalar2=None, op0=ALU.mult), reads=[COWN], writes=[COWN])
        K.op("dve", lambda: nc.vector.tensor_copy(out=qp[0][0:8, :], in_=Cown[0:8, :]), reads=[COWN], writes=[QP])
        K.op("dve", lambda: nc.vector.tensor_tensor(out=Cown[0:8, :], in0=Cown[0:8, :], in1=qp[0][0:8, :], op=ALU.subtract), reads=[COWN, QP], writes=[COWN])
        K.op("dve", lambda: nc.vector.tensor_copy(out=qp[1][0:8, :], in_=Cown[0:8, :]), reads=[COWN], writes=[QP])
        K.op("dve", lambda: nc.vector.tensor_tensor(out=Cown[0:8, :], in0=Cown[0:8, :], in1=qp[1][0:8, :], op=ALU.subtract), reads=[COWN, QP], writes=[COWN])
        K.op("dve", lambda: nc.vector.tensor_copy(out=qp[2][0:8, :], in_=Cown[0:8, :]), reads=[COWN], writes=[QP])
        for i in range(3):
            K.dma("sp", cs_k.ap()[i], kp[i][0:8, :], reads=[KP], writes=[CSK])
            K.dma("sp", cs_q.ap()[i], qp[i][0:8, :], reads=[QP], writes=[CSQ])

    phase25()
    K.barrier(skip_cc=True)

    def phase3():
        so = S_OFF
        kt = [AR.carve(so + i * 8192, [8, 512], BF16) for i in range(2)]
        vv = [AR.carve(so + 16384 + i * 8192, [32, 128], BF16) for i in range(2)]
        kaug = [AR.carve(so + 32768 + i * 8224, [LSEQ], BF16) for i in range(2)]
        qaug = [AR.carve(so + 49216 + i * 2048, [TO], BF16) for i in range(2)]
        Pb = [AR.carve(so + 53312 + i * 1024, [512], BF16) for i in range(2)]
        Eb = [AR.carve(so + 55360 + i * 2048, [512], F32) for i in range(2)]
        Lb = [AR.carve(so + 59456 + i * 1024, [512], BF16) for i in range(2)]
        Tb = [AR.carve(so + 61504 + i * 2048, [512], F32) for i in range(2)]
        Xb = [AR.carve(so + 65600 + i * 2048, [512], F32) for i in range(2)]
        Wb = [AR.carve(so + 69696 + i * 1024, [512], BF16) for i in range(2)]
        rden = AR.carve(so + 71744, [512], F32)
        KTC = [[K.res("kt%d_%d" % (t, c)) for c in range(8)] for t in range(2)]
        VVC = [[K.res("vv%d_%d" % (t, c)) for c in range(8)] for t in range(2)]
        KAUG = [K.res("kaug0"), K.res("kaug1")]; QAUG = [K.res("qaug0"), K.res("qaug1")]
        PB = [K.res("pb0"), K.res("pb1")]; EB = [K.res("eb0"), K.res("eb1")]; LB = [K.res("lb0"), K.res("lb1")]
        TB = [K.res("tb0"), K.res("tb1")]; XB = [K.res("xb0"), K.res("xb1")]; WB_ = [K.res("wb0"), K.res("wb1")]
        RDEN = K.res("rden")
        for i in range(2):
            K.op("dve", lambda: nc.vector.memset(kaug[i][:, :], 0.0), writes=[KAUG[i]])
            K.op("dve", lambda: nc.vector.memset(qaug[i][:, :], 0.0), writes=[QAUG[i]])
            K.op("dve", lambda: nc.vector.memset(kaug[i][0:8, :], 1.0), writes=[KAUG[i]])
            K.op("dve", lambda: nc.vector.memset(qaug[i][0:8, :], 1.0), writes=[QAUG[i]])
            K.dma("sp", kaug[i][6:8, :], ckmask.ap(), writes=[KAUG[i]])
            K.dma("sp", qaug[i][6:8, :], cqsel.ap(), writes=[QAUG[i]])

        def load_chunk(t, typ, h, ch):
            rank, pos = chunk_src(ch)
            K.dma("sp", kt[t][:, ch, :], agout_k[typ][pos].ap()[rank * 1024 + h * 128:rank * 1024 + (h + 1) * 128, :],
                  reads=[AGOUT_K[typ][pos]], writes=[KTC[t][ch]])
            K.dma("sp", vv[t][:, 4 * ch:4 * ch + 4, :],
                  agout_v[typ][pos].ap()[rank * 512:(rank + 1) * 512, h * 128:(h + 1) * 128].rearrange("(kb p) d -> p kb d", p=128),
                  reads=[AGOUT_V[typ][pos]], writes=[VVC[t][ch]])

        def load_pair(h):
            ab = h % 2
            K.dma("sp", kaug[ab][0:3, :], cs_k.ap()[:, h, :], reads=[CSK], writes=[KAUG[ab]])
            K.dma("sp", qaug[ab][3:6, :], cs_q.ap()[:, h, :], reads=[CSQ], writes=[QAUG[ab]])
            for c in range(8):
                load_chunk(0, "f", h, c)
                load_chunk(1, "s", h, [3, 2, 1, 0, 7, 6, 5, 4][c] if h == 0 else 7 - c)

        def blocks_for(qt):
            chunks = range(4) if qt == 0 else range(8)
            bl = [("meta", 0, 0)]
            for ch in chunks:
                for kb in range(4):
                    bl.append(("blk", ch, kb))
            return bl

        def unit_of(qt, ch):
            if qt == 0:
                return ch
            return ch if ch >= 4 else None

        def blk_aps(t, h, b):
            kind, ch, kb = b
            if kind == "meta":
                mk = metaKf if t == 0 else metaKs
                mv = metaVf if t == 0 else metaVs
                return NM, mk[:, h, :], mv[0:NM, h * 128:(h + 1) * 128], 0, [METAR], [METAR]
            return (128, kt[t][:, ch, kb * 128:(kb + 1) * 128], vv[t][:, 4 * ch + kb, :], NM + ch * 512 + kb * 128,
                    [KTC[t][ch]], [VVC[t][ch]])

        def pair(h):
            ab = h % 2
            for qt in range(2):
                q0 = qt * 512
                blf = blocks_for(qt)
                bls = list(reversed(blf[1:])) + [blf[0]]
                n = len(blf)

                def s_block(i):
                    kind, ch, kb = blf[i]
                    nk, kT, vB, col, kres, vres = blk_aps(0, h, blf[i])
                    si = i % 2
                    u = unit_of(qt, ch) if kind == "blk" else None
                    K.op("pe", lambda: nc.tensor.matmul(PS[si][0:nk, :], kT, qf[:, h, q0:q0 + 512], start=True, stop=False),
                         reads=kres + [QF], writes=[PSR[si]], mark=False)
                    K.op("pe", lambda: nc.tensor.matmul(PS[si][0:nk, :], kaug[ab][:, col:col + nk], qaug[ab][:, q0:q0 + 512], start=False, stop=(u is None)),
                         reads=[KAUG[ab], QAUG[ab], QF] + kres, writes=[PSR[si]], mark=(u is None))
                    if u is not None:
                        K.op("pe", lambda: nc.tensor.matmul(PS[si][0:nk, :], isel_sb[:, u, :], diag_sb[:, kb, :], start=False, stop=True),
                             reads=[CONST, KAUG[ab], QAUG[ab], QF] + kres, writes=[PSR[si]])
                    K.op("act", lambda: nc.scalar.activation(out=Pb[si][0:nk, :], in_=PS[si][0:nk, :], func=AF.Exp), reads=[PSR[si]], writes=[PB[si]])

                def pv_block(i):
                    nk, kT, vB, col, kres, vres = blk_aps(0, h, blf[i])
                    si = i % 2
                    K.op("pe", lambda: nc.tensor.matmul(PS[2][:, :], vB, Pb[si][0:nk, :], start=(i == 0), stop=(i == n - 1)),
                         reads=vres + [PB[si]], writes=[PSR[2]], mark=False)
                    K.op("pe", lambda: nc.tensor.matmul(PS[3][:, :], ones_b[0:nk, :], Pb[si][0:nk, :], start=(i == 0), stop=(i == n - 1)),
                         reads=[CONST, PB[si]] + vres, writes=[PSR[3], PSR[2]])

                def z_block(i):
                    kind, ch, kb = bls[i]
                    nk, kT, vB, col, kres, vres = blk_aps(1, h, bls[i])
                    zi = 4 + i % 2
                    j = i % 2
                    u = unit_of(qt, ch) if kind == "blk" else None
                    K.op("pe", lambda: nc.tensor.matmul(PS[zi][0:nk, :], kT, qs[:, h, q0:q0 + 512], start=True, stop=(u is None)),
                         reads=kres + [QS], writes=[PSR[zi]], mark=(u is None))
                    if u is not None:
                        K.op("pe", lambda: nc.tensor.matmul(PS[zi][0:nk, :], kmask_sb[:, col:col + nk], qsel_sb[:, q0:q0 + 512], start=False, stop=False),
                             reads=[CONST], writes=[PSR[zi]], mark=False)
                        K.op("pe", lambda: nc.tensor.matmul(PS[zi][0:nk, :], isel_sb[:, u, :], diag_sb[:, 4 + kb, :], start=False, stop=True),
                             reads=[CONST, QS] + kres, writes=[PSR[zi]])
                    K.op("act", lambda: nc.scalar.activation(out=Eb[j][0:nk, :], in_=PS[zi][0:nk, :], func=AF.Exp), reads=[PSR[zi]], writes=[EB[j]])
                    K.op("act", lambda: nc.scalar.activation(out=Lb[j][0:nk, :], in_=Eb[j][0:nk, :], func=AF.Ln, bias=1.0), reads=[EB[j]], writes=[LB[j]])

                def tril(i):
                    nk = NM if bls[i][0] == "meta" else 128
                    j = i % 2
                    K.op("pe", lambda: nc.tensor.matmul(PS[6][0:nk, :], tri_b[0:nk, 0:nk], Lb[j][0:nk, :], start=(i == 0), stop=(i == n - 1)),
                         reads=[LB[j], CONST], writes=[PSR[6]])
                    K.op("act", lambda: nc.scalar.activation(out=Xb[j][0:nk, :], in_=PS[6][0:nk, :], func=AF.Exp, scale=-1.0), reads=[PSR[6]], writes=[XB[j]])

                def tric_w(i):
                    nk = NM if bls[i][0] == "meta" else 128
                    j = i % 2
                    if i < n - 1:
                        K.op("pe", lambda: nc.tensor.matmul(PS[6][:, :], tric_b[0:nk, :], Lb[j][0:nk, :], start=False, stop=False),
                             reads=[LB[j], CONST], writes=[PSR[6]])
                    K.op("dve", lambda: nc.vector.tensor_tensor(out=Wb[j][0:nk, :], in0=Eb[j][0:nk, :], in1=Xb[j][0:nk, :], op=ALU.mult),
                         reads=[EB[j], XB[j]], writes=[WB_[j]])

                def pv_s(i):
                    nk, kT, vB, col, kres, vres = blk_aps(1, h, bls[i])
                    j = i % 2
                    K.op("pe", lambda: nc.tensor.matmul(PS[7][:, :], vB, Wb[j][0:nk, :], start=(i == 0), stop=(i == n - 1)),
                         reads=vres + [WB_[j]], writes=[PSR[7]])

                z_block(0)
                s_block(0)
                for i in range(n):
                    tril(i)
                    if i + 1 < n:
                        z_block(i + 1)
                        s_block(i + 1)
                    tric_w(i)
                    pv_block(i)
                    if i > 0:
                        pv_s(i - 1)
                pv_s(n - 1)
                K.op("dve", lambda: nc.vector.reciprocal(out=rden, in_=PS[3][:, :]), reads=[PSR[3]], writes=[RDEN])
                K.op("dve", lambda: nc.vector.tensor_tensor(out=of[:, h, q0:q0 + 512], in0=PS[2][:, :], in1=rden, op=ALU.mult),
                     reads=[PSR[2], PSR[3], RDEN], writes=[OF])
                K.op("dve", lambda: nc.vector.tensor_copy(out=osb[:, h, q0:q0 + 512], in_=PS[7][:, :]),
                     reads=[PSR[7]], writes=[OS])

        load_pair(0)
        for h in range(8):
            pair(h)
            if h + 1 < 8:
                load_pair(h + 1)

    phase3()
    K.barrier()

    merged = AR.carve(S_OFF, [DC, TO], BF16)
    MERGED = K.res("merged")
    wbf_v = wbf.ap().rearrange("(kc p) d -> p kc d", p=128)
    wbs_v = wbs.ap().rearrange("(kc p) d -> p kc d", p=128)

    def phase4a():
        so = S_OFF + 32768
        wgf = [AR.carve(so + i * 4096, [DC, 128], BF16) for i in range(2)]
        wgs = [AR.carve(so + 8192 + i * 4096, [DC, 128], BF16) for i in range(2)]
        wbfb = [AR.carve(so + 16384 + i * 2048, [8, 128], BF16) for i in range(2)]
        wbsb = [AR.carve(so + 20480 + i * 2048, [8, 128], BF16) for i in range(2)]
        sgf = [AR.carve(so + 24576 + i * 2048, [512], F32) for i in range(2)]
        sgs = [AR.carve(so + 28672 + i * 2048, [512], F32) for i in range(2)]
        W4 = [K.res("w4_0"), K.res("w4_1")]
        SGF = [K.res("sgf0"), K.res("sgf1")]; SGS = [K.res("sgs0"), K.res("sgs1")]

        def load(m):
            s = m % 2
            K.dma("pool", wgf[s], win_v[:, :, C_GF + m * 128:C_GF + (m + 1) * 128], writes=[W4[s]])
            K.dma("pool", wgs[s], win_v[:, :, C_GS + m * 128:C_GS + (m + 1) * 128], writes=[W4[s]])
            K.dma("pool", wbfb[s], wbf_v[:, :, m * 128:(m + 1) * 128], writes=[W4[s]])
            K.dma("pool", wbsb[s], wbs_v[:, :, m * 128:(m + 1) * 128], writes=[W4[s]])

        load(0); load(1)
        cnt = 0
        for m in range(DC):
            s = m % 2
            for ti, (t0, tn) in enumerate(TILES2):
                pb = 4 * (cnt % 2)
                j = cnt % 2
                cnt += 1
                for kc in range(8):
                    K.op("pe", lambda: nc.tensor.matmul(PS[pb][:, :], wbfb[s][:, kc, :], of[:, kc, t0:t0 + tn], start=(kc == 0), stop=(kc == 7)),
                         reads=[W4[s], OF], writes=[PSR[pb]], mark=(kc == 7))
                for c in range(DC):
                    K.op("pe", lambda: nc.tensor.matmul(PS[pb + 1][:, :], wgf[s][:, c, :], xn[:, c, t0:t0 + tn], start=(c == 0), stop=(c == DC - 1)),
                         reads=[W4[s]] + xnr(t0, tn), writes=[PSR[pb + 1]], mark=(c == DC - 1))
                for kc in range(8):
                    K.op("pe", lambda: nc.tensor.matmul(PS[pb + 2][:, :], wbsb[s][:, kc, :], osb[:, kc, t0:t0 + tn], start=(kc == 0), stop=(kc == 7)),
                         reads=[W4[s], OS], writes=[PSR[pb + 2]], mark=(kc == 7))
                for c in range(DC):
                    K.op("pe", lambda: nc.tensor.matmul(PS[pb + 3][:, :], wgs[s][:, c, :], xn[:, c, t0:t0 + tn], start=(c == 0), stop=(c == DC - 1)),
                         reads=[W4[s]] + xnr(t0, tn), writes=[PSR[pb + 3]], mark=(c == DC - 1))
                K.op("act", lambda: nc.scalar.activation(out=sgf[j], in_=PS[pb + 1][:, :], func=AF.Sigmoid), reads=[PSR[pb + 1]], writes=[SGF[j]])
                K.op("act", lambda: nc.scalar.activation(out=sgs[j], in_=PS[pb + 3][:, :], func=AF.Sigmoid), reads=[PSR[pb + 3]], writes=[SGS[j]])
                K.op("dve", lambda: nc.vector.tensor_tensor(out=sgf[j], in0=sgf[j], in1=PS[pb][:, :], op=ALU.mult), reads=[SGF[j], PSR[pb]], writes=[SGF[j]])
                K.op("dve", lambda: nc.vector.tensor_tensor(out=sgs[j], in0=sgs[j], in1=PS[pb + 2][:, :], op=ALU.mult), reads=[SGS[j], PSR[pb + 2]], writes=[SGS[j]])
                K.op("dve", lambda: nc.vector.tensor_tensor(out=merged[:, m, t0:t0 + tn], in0=sgf[j], in1=sgs[j], op=ALU.add), reads=[SGF[j], SGS[j]], writes=[MERGED])
            if m + 2 < DC:
                load(m + 2)

    phase4a()
    K.barrier()
    for c4 in range(4):
        K.dma("sp", h[:, 4 * c4:4 * c4 + 4, 0:TO], hsp_v[:, 4 * c4:4 * c4 + 4, :], reads=[HSP], writes=[HALL])
    for c in range(DC):
        for t in range(2):
            HR[c][t].w = HALL.w
            HR[c][t].r = {}

    def phase4b():
        so = S_OFF + 32768
        wos = [AR.carve(so + i * 8192, [DC, 256], BF16) for i in range(2)]
        WO = [K.res("wo0"), K.res("wo1")]
        wo_v = wo.ap().rearrange("(c p) f -> p c f", p=128)

        def load(g):
            K.dma("pool", wos[g % 2], wo_v[:, :, g * 256:(g + 1) * 256], writes=[WO[g % 2]])

        load(0); load(1)
        cnt = 0
        for g in range(8):
            s = g % 2
            for mc in range(2):
                m2 = 2 * g + mc
                for ti, (t0, tn) in enumerate(TILES2):
                    pb = cnt % 4
                    cnt += 1
                    for m in range(DC):
                        K.op("pe", lambda: nc.tensor.matmul(PS[pb][:, :], wos[s][:, m, mc * 128:(mc + 1) * 128], merged[:, m, t0:t0 + tn], start=(m == 0), stop=(m == DC - 1)),
                             reads=[WO[s], MERGED], writes=[PSR[pb]], mark=(m == DC - 1))
                    K.op("dve", lambda: nc.vector.tensor_tensor(out=h[:, m2, t0:t0 + tn], in0=PS[pb][:, :], in1=h[:, m2, t0:t0 + tn], op=ALU.add),
                         reads=[PSR[pb], HR[m2][ti]], writes=[HR[m2][ti]])
            if g + 2 < 8:
                load(g + 2)

    phase4b()
    K.barrier()

    if FULL:
        rmsnorm(g3_sb, TILES2, S_OFF + 61568)
        ffn(w2g, w2u, w2d, TILES2, S_OFF)
        K.barrier()

    OUT = K.res("out")
    outT_v = outT.ap().rearrange("(c p) t -> p c t", p=128)
    for c4 in range(4):
        K.dma("sp", outT_v[:, 4 * c4:4 * c4 + 4, :], h[:, 4 * c4:4 * c4 + 4, 0:TO], writes=[OUT])
    K.wait("sp", OUT.w)
    return nc


def _bf(a):
    return np.ascontiguousarray(a).astype(ml_dtypes.bfloat16)


def make_consts():
    ones = np.ones((128, 128), np.float32)
    ident = np.eye(128, dtype=np.float32)
    sp = np.arange(128)[:, None]; s = np.arange(128)[None, :]
    tri = (sp >= s).astype(np.float32)
    tric = (sp < s).astype(np.float32)
    cmat = np.stack([ones, ident, tri, tric], axis=1)
    p = np.arange(128)[:, None]; f = np.arange(512)[None, :]
    diag = np.zeros((128, 8, 512), np.float32)
    for kb in range(4):
        diag[:, kb, :] = np.where(kb * 128 + p <= f, 0.0, NEG)
        diag[:, 4 + kb, :] = np.where(kb * 128 + p < f, 0.0, NEG)
    qsel = np.zeros((2, TO), np.float32)
    qsel[0, :512] = 1.0
    qsel[1, 512:] = 1.0
    return dict(cmat=_bf(cmat), cones=ones, cdiag=_bf(diag), cqsel=_bf(qsel))


def core_consts(j):
    A, B = j, 7 - j
    isel = np.zeros((128, 8, 128), np.float32)
    isel[:, A, :] = np.eye(128)
    isel[:, 4 + (B - 4), :] = np.eye(128)
    kmask = np.zeros((2, LSEQ), np.float32)
    kmask[0, NM + (A + 1) * 512:] = NEG
    kmask[1, NM + (B + 1) * 512:] = NEG
    presel = np.zeros((8, 2, 9), np.float32)
    presel[:, 0, A] = 1.0
    presel[:, 1, B] = 1.0
    return dict(cisel=_bf(isel), ckmask=_bf(kmask), cpresel=presel)


_NC_CACHE = {}


def kernel(x, meta_tokens, ffn1_norm, ffn1_w_gate, ffn1_w_up, ffn1_w_down, mix_norm, w_in, b_forget,
           fox_q_norm, fox_k_norm, w_branch_fox, w_branch_sb, w_out, ffn2_norm, ffn2_w_gate, ffn2_w_up,
           ffn2_w_down, _mode="full", _h1=None, _ncores=8):
    f32 = lambda a: np.ascontiguousarray(np.asarray(a, dtype=np.float32))
    x = f32(x); meta = f32(meta_tokens)
    gl = lambda g: f32(np.asarray(g)[0].reshape(DC, 128).T)
    common = dict(
        g1=gl(ffn1_norm), g2=gl(mix_norm), g3=gl(ffn2_norm),
        gq=f32(np.asarray(fox_q_norm)[0].T), gk=f32(np.asarray(fox_k_norm)[0].T),
        bfo=f32(np.asarray(b_forget)[0].reshape(8, 1)),
        w1g=f32(np.asarray(ffn1_w_gate)[0]), w1u=f32(np.asarray(ffn1_w_up)[0]), w1d=f32(np.asarray(ffn1_w_down)[0]),
        w2g=f32(np.asarray(ffn2_w_gate)[0]), w2u=f32(np.asarray(ffn2_w_up)[0]), w2d=f32(np.asarray(ffn2_w_down)[0]),
        win=f32(np.asarray(w_in)[0]), wbf=f32(np.asarray(w_branch_fox)[0]), wbs=f32(np.asarray(w_branch_sb)[0]),
        wo=f32(np.asarray(w_out)[0]),
    )
    common.update(make_consts())
    in_maps = []
    for core in range(_ncores):
        b, j = core // 4, core % 4
        A, B = j, 7 - j
        if _h1 is None:
            own = np.concatenate([x[b, A * 512:(A + 1) * 512], x[b, B * 512:(B + 1) * 512], meta], axis=0)
        else:
            hb = _h1[b]
            own = np.concatenate([hb[16 + A * 512:16 + (A + 1) * 512], hb[16 + B * 512:16 + (B + 1) * 512], hb[0:16]], axis=0)
        m = dict(common)
        m["xT"] = np.ascontiguousarray(own.T)
        m.update(core_consts(j))
        in_maps.append(m)
    key = (_mode, _ncores)
    if key not in _NC_CACHE:
        _NC_CACHE[key] = build(_mode, _ncores // 4)
    nc = _NC_CACHE[key]
    if _mode != "full":
        for m in in_maps:
            for k in ("w1g", "w1u", "w1d", "w2g", "w2u", "w2d"):
                m.pop(k, None)
    res = run_bass_kernel_spmd(nc, in_maps, core_ids=list(range(_ncores)))
    out = np.zeros((2, 4096, D), np.float32)
    for core in range(_ncores):
        b, j = core // 4, core % 4
        A, B = j, 7 - j
        o = np.asarray(res.results[core]["outT"]).T
        out[b, A * 512:(A + 1) * 512] = o[:512]
        out[b, B * 512:(B + 1) * 512] = o[512:]
    return out
```
